# Optimizing a Trainium2 kernel written in Bass

```python
import math
import jax, jax.numpy as jnp
from jax import lax
import numpy as np

D_MODEL = 1024
BATCH = 8
SEQ = 2048
DEPTH = 2

CHUNK = 64
N_MIXERS = 2
SB_HEADS = 16
SB_HEAD_DIM = D_MODEL // SB_HEADS
Q_BLOCK = 128
LRU_WIDTH = D_MODEL
LRU_BLOCKS = 16
LRU_BLOCK_W = LRU_WIDTH // LRU_BLOCKS
LRU_C = 8.0
CONV_W = 4
D_FF = 2816
NORM_EPS = 1e-6

kernel_name = "macaron_stickbreak_rglru_hybrid"


def rms_norm(x, gain):
    xf = x.astype(jnp.float32)
    y = xf * lax.rsqrt(jnp.mean(xf * xf, axis=-1, keepdims=True) + NORM_EPS)
    return (y * gain.astype(jnp.float32)).astype(x.dtype)


def swiglu(xn, w_in, w_out):
    gate, up = jnp.split(xn @ w_in, 2, axis=-1)
    return (jax.nn.silu(gate) * up) @ w_out


def stick_breaking_attention(xn, w_qkv, q_gain, k_gain, w_o):
    b, s, _ = xn.shape
    qkv = (xn @ w_qkv).reshape(b, s, 3, SB_HEADS, SB_HEAD_DIM)
    q = rms_norm(qkv[:, :, 0], q_gain).astype(jnp.float32).transpose(0, 2, 1, 3)
    k = rms_norm(qkv[:, :, 1], k_gain).astype(jnp.float32).transpose(0, 2, 1, 3)
    v = qkv[:, :, 2].astype(jnp.float32).transpose(0, 2, 1, 3)
    scale = 1.0 / math.sqrt(SB_HEAD_DIM)
    outs = []
    for blk in range(s // Q_BLOCK):
        lo = blk * Q_BLOCK
        hi = lo + Q_BLOCK
        z = jnp.einsum('bhqd,bhkd->bhqk', q[:, :, lo:hi], k[:, :, :hi]) * scale
        t_idx = lo + jnp.arange(Q_BLOCK)[:, None]
        s_idx = jnp.arange(hi)[None, :]
        mask = s_idx < t_idx
        log_beta = jax.nn.log_sigmoid(z)
        log_1m = jnp.where(mask, jax.nn.log_sigmoid(-z), 0.0)
        rev = lax.cumsum(log_1m, axis=3, reverse=True)
        rev_excl = jnp.concatenate([rev[..., 1:], jnp.zeros_like(rev[..., :1])], axis=-1)
        w = jnp.where(mask, jnp.exp(log_beta + rev_excl), 0.0)
        outs.append(jnp.einsum('bhqk,bhkd->bhqd', w, v[:, :, :hi]))
    o = jnp.concatenate(outs, axis=2)
    o = o.transpose(0, 2, 1, 3).reshape(b, s, SB_HEADS * SB_HEAD_DIM).astype(xn.dtype)
    return o @ w_o


def rglru_block(xn, w_in, conv_w, conv_b, w_r, b_r, w_i, b_i, lam, w_o):
    b, s, _ = xn.shape
    xb, yb = jnp.split(xn @ w_in, 2, axis=-1)
    y = jax.nn.gelu(yb, approximate=True)
    xpad = jnp.pad(xb, ((0, 0), (CONV_W - 1, 0), (0, 0)))
    xc = conv_b + sum(conv_w[j] * xpad[:, j:j + s] for j in range(CONV_W))
    xh = xc.reshape(b, s, LRU_BLOCKS, LRU_BLOCK_W)
    r = jax.nn.sigmoid(jnp.einsum('bsnc,ncd->bsnd', xh, w_r).reshape(b, s, LRU_WIDTH) + b_r)
    i = jax.nn.sigmoid(jnp.einsum('bsnc,ncd->bsnd', xh, w_i).reshape(b, s, LRU_WIDTH) + b_i)
    log_a = LRU_C * r.astype(jnp.float32) * jax.nn.log_sigmoid(lam.astype(jnp.float32))
    a = jnp.exp(log_a)
    mult = jnp.sqrt(-jnp.expm1(2.0 * log_a))
    u = mult * (i * xc).astype(jnp.float32)

    def combine(left, right):
        a_l, h_l = left
        a_r, h_r = right
        return a_l * a_r, a_r * h_l + h_r

    _, h = lax.associative_scan(combine, (a, u), axis=1)
    return (h.astype(xn.dtype) * y) @ w_o


def setup_inputs(seed: int = 0) -> dict:
    key = jax.random.key(seed)
    keys = iter(jax.random.split(key, 40))

    def normal(shape, scale):
        return jax.random.normal(next(keys), shape, jnp.float32) * scale

    def gain(n):
        return 1.0 + normal((n,), 0.02)

    def lam_init(n):
        a_c = jax.random.uniform(next(keys), (n,), jnp.float32, 0.9, 0.999)
        a = a_c ** (1.0 / LRU_C)
        return jnp.log(a) - jnp.log1p(-a)

    d, f, w = D_MODEL, D_FF, LRU_WIDTH
    p = {}
    p["x"] = normal((BATCH, SEQ, d), 1.0)
    p["l0_ff1_norm"] = gain(d)
    p["l0_ff1_w_in"] = normal((d, 2 * f), d ** -0.5)
    p["l0_ff1_w_out"] = normal((f, d), f ** -0.5)
    p["l0_mix_norm"] = gain(d)
    p["l0_sb_w_qkv"] = normal((d, 3 * SB_HEADS * SB_HEAD_DIM), d ** -0.5)
    p["l0_sb_q_norm"] = gain(SB_HEAD_DIM)
    p["l0_sb_k_norm"] = gain(SB_HEAD_DIM)
    p["l0_sb_w_o"] = normal((SB_HEADS * SB_HEAD_DIM, d), d ** -0.5)
    p["l0_ff2_norm"] = gain(d)
    p["l0_ff2_w_in"] = normal((d, 2 * f), d ** -0.5)
    p["l0_ff2_w_out"] = normal((f, d), f ** -0.5)
    p["l1_ff1_norm"] = gain(d)
    p["l1_ff1_w_in"] = normal((d, 2 * f), d ** -0.5)
    p["l1_ff1_w_out"] = normal((f, d), f ** -0.5)
    p["l1_mix_norm"] = gain(d)
    p["l1_lru_w_in"] = normal((d, 2 * w), d ** -0.5)
    p["l1_lru_conv_w"] = normal((CONV_W, w), CONV_W ** -0.5)
    p["l1_lru_conv_b"] = normal((w,), 0.01)
    p["l1_lru_w_r"] = normal((LRU_BLOCKS, LRU_BLOCK_W, LRU_BLOCK_W), LRU_BLOCK_W ** -0.5)
    p["l1_lru_b_r"] = normal((w,), 0.01)
    p["l1_lru_w_i"] = normal((LRU_BLOCKS, LRU_BLOCK_W, LRU_BLOCK_W), LRU_BLOCK_W ** -0.5)
    p["l1_lru_b_i"] = normal((w,), 0.01)
    p["l1_lru_lambda"] = lam_init(w)
    p["l1_lru_w_o"] = normal((w, d), w ** -0.5)
    p["l1_ff2_norm"] = gain(d)
    p["l1_ff2_w_in"] = normal((d, 2 * f), d ** -0.5)
    p["l1_ff2_w_out"] = normal((f, d), f ** -0.5)
    return p


def reference(x,
              l0_ff1_norm, l0_ff1_w_in, l0_ff1_w_out,
              l0_mix_norm, l0_sb_w_qkv, l0_sb_q_norm, l0_sb_k_norm, l0_sb_w_o,
              l0_ff2_norm, l0_ff2_w_in, l0_ff2_w_out,
              l1_ff1_norm, l1_ff1_w_in, l1_ff1_w_out,
              l1_mix_norm, l1_lru_w_in, l1_lru_conv_w, l1_lru_conv_b,
              l1_lru_w_r, l1_lru_b_r, l1_lru_w_i, l1_lru_b_i, l1_lru_lambda, l1_lru_w_o,
              l1_ff2_norm, l1_ff2_w_in, l1_ff2_w_out):
    ffn1 = [(l0_ff1_norm, l0_ff1_w_in, l0_ff1_w_out),
            (l1_ff1_norm, l1_ff1_w_in, l1_ff1_w_out)]
    ffn2 = [(l0_ff2_norm, l0_ff2_w_in, l0_ff2_w_out),
            (l1_ff2_norm, l1_ff2_w_in, l1_ff2_w_out)]
    mixers = [(l0_mix_norm, (l0_sb_w_qkv, l0_sb_q_norm, l0_sb_k_norm, l0_sb_w_o)),
              (l1_mix_norm, (l1_lru_w_in, l1_lru_conv_w, l1_lru_conv_b, l1_lru_w_r, l1_lru_b_r,
                             l1_lru_w_i, l1_lru_b_i, l1_lru_lambda, l1_lru_w_o))]
    for layer in range(DEPTH):
        n1, wi1, wo1 = ffn1[layer]
        x = x + 0.5 * swiglu(rms_norm(x, n1), wi1, wo1)
        mix_norm, mix_params = mixers[layer]
        xn = rms_norm(x, mix_norm)
        if layer % N_MIXERS == 0:
            x = x + stick_breaking_attention(xn, *mix_params)
        else:
            x = x + rglru_block(xn, *mix_params)
        n2, wi2, wo2 = ffn2[layer]
        x = x + 0.5 * swiglu(rms_norm(x, n2), wi2, wo2)
    return x
```

```python
import numpy as np
from contextlib import ExitStack
import concourse.bass as bass
import concourse.mybir as mybir
from concourse.bass_utils import run_bass_kernel_spmd

F32 = mybir.dt.float32
BF16 = mybir.dt.bfloat16
AF = mybir.ActivationFunctionType
ALU = mybir.AluOpType

S = 2048
D = 1024
KC = 8
NT = 4
TW = 512
FF = 2816
NFC = 22
EPS = 1e-6

V_NORM = {"l0_ff1_norm": 0, "l0_mix_norm": 8, "l0_ff2_norm": 16,
          "l1_ff1_norm": 24, "l1_mix_norm": 32, "l1_ff2_norm": 40}
V_CONVW = 48
V_CONVB = 80
V_BR = 88
V_BI = 96
V_LAM = 104
V_QG = 112
V_KG = 113
NV = 114


class Buf:
    __slots__ = ("name", "w", "r", "dsem", "dcnt")

    def __init__(self, name):
        self.name = name
        self.w = None
        self.r = {}
        self.dsem = None
        self.dcnt = 0


class Eng:
    def __init__(self, name, sem):
        self.name = name
        self.sem = sem
        self.cnt = 0
        self.prog = []
        self.waited = {}


class Sched:
    def __init__(self, nc, es):
        self.nc = nc
        self.es = es
        self.nsem = 0
        self.pe = Eng("pe", self.new_sem("pe"))
        self.act = Eng("act", self.new_sem("act"))
        self.dve = Eng("dve", self.new_sem("dve"))
        self.pool = Eng("pool", self.new_sem("pool"))
        self.sp = Eng("sp", self.new_sem("sp"))
        self.engs = [self.pe, self.act, self.dve, self.pool, self.sp]

    def new_sem(self, name="s"):
        self.nsem += 1
        return self.es.enter_context(self.nc.semaphore("%s_%d" % (name, self.nsem)))

    def _deps(self, eng, reads, writes):
        toks = []
        for b in reads:
            if b.w is not None:
                toks.append(b.w)
        for b in writes:
            if b.w is not None:
                toks.append(b.w)
            toks.extend(b.r.values())
        for (sem, val) in toks:
            key = id(sem)
            if eng.waited.get(key, 0) >= val:
                continue
            eng.waited[key] = val
            eng.prog.append(("wait", sem, val))

    @staticmethod
    def _mark(tok, reads, writes):
        key = id(tok[0])
        for b in reads:
            old = b.r.get(key)
            if old is None or old[1] < tok[1]:
                b.r[key] = tok
        for b in writes:
            b.w = tok
            b.r = {}

    def op(self, eng, fn, reads=(), writes=()):
        self._deps(eng, reads, writes)
        eng.cnt += 1
        tok = (eng.sem, eng.cnt)
        eng.prog.append(("op", fn, eng.sem))
        self._mark(tok, reads, writes)
        return tok

    def dma(self, eng, pairs, reads=(), writes=(), sem_buf=None):
        self._deps(eng, reads, writes)
        b = sem_buf if sem_buf is not None else writes[0]
        if b.dsem is None:
            b.dsem = self.new_sem("d")
        for (o, i) in pairs:
            b.dcnt += 16
            eng.prog.append(("dma", o, i, b.dsem))
        tok = (b.dsem, b.dcnt)
        self._mark(tok, reads, writes)
        return tok

    def wait_tok(self, eng, tok):
        key = id(tok[0])
        if eng.waited.get(key, 0) >= tok[1]:
            return
        eng.waited[key] = tok[1]
        eng.prog.append(("wait", tok[0], tok[1]))

    def barrier(self, engs=None):
        engs = engs or [self.pe, self.act, self.dve]
        toks = [(e.sem, e.cnt) for e in engs if e.cnt > 0]
        for e in engs:
            for t in toks:
                if t[0] is e.sem:
                    continue
                self.wait_tok(e, t)

    @staticmethod
    def replay(eng, e):
        for item in eng.prog:
            if item[0] == "wait":
                e.wait_ge(item[1], item[2])
            elif item[0] == "op":
                ins = item[1](e)
                ins.then_inc(item[2], 1)
            else:
                e.dma_start(out=item[1], in_=item[2]).then_inc(item[3], 16)


class Arena:
    def __init__(self, t, nelem):
        self.t = t
        self.n = nelem
        self.off = 0

    def reset(self):
        self.off = 0

    def alloc(self, shape, dtype):
        n = 1
        for s in shape:
            n *= s
        ne = n * (2 if dtype == F32 else 1)
        ne = (ne + 31) // 32 * 32
        assert self.off + ne <= self.n, ("arena overflow", self.off, ne, self.n)
        ap = self.t[:, self.off:self.off + ne]
        self.off += ne
        if dtype == F32:
            ap = ap.bitcast(F32)
        ap = ap[:, 0:n]
        if len(shape) == 2:
            ap = ap.rearrange("p (a b) -> p a b", b=shape[1])
        elif len(shape) == 3:
            ap = ap.rearrange("p (a b c) -> p a b c", b=shape[1], c=shape[2])
        return ap


class Prog:
    def __init__(self, phases):
        self.phases = phases
        nc = self.nc = bass.Bass("TRN2", target_bir_lowering=False)
        es = self.es = ExitStack()
        self.sc = Sched(nc, es)
        d = self.dram = {}

        def din(name, shape):
            d[name] = nc.dram_tensor(name, list(shape), F32, kind="ExternalInput").ap()

        din("x", [128, KC, S])
        din("vecs", [128, NV])
        din("consts", [128, 3, 128])
        for l in (0, 1):
            for f in ("ff1", "ff2"):
                din("l%d_%s_win" % (l, f), [NFC, 128, KC, 256])
                din("l%d_%s_wout" % (l, f), [128, NFC, D])
        din("wqkv", [8, 128, KC, 384])
        din("wo_sb", [128, KC, D])
        din("lru_win", [8, 128, KC, 256])
        din("lru_wr", [128, 8, 64])
        din("lru_wi", [128, 8, 64])
        din("lru_wo", [128, KC, D])
        self.y = nc.dram_tensor("y", [128, KC, S], F32, kind="ExternalOutput").ap()

        sb = lambda name, shape, dt: es.enter_context(nc.sbuf_tensor(name, shape, dt))
        self.X = sb("X", [128, KC, S], F32)
        self.XN = sb("XN", [128, KC, S], BF16)
        self.VEC = sb("VEC", [128, NV], F32)
        self.CST = sb("CST", [128, 3, 128], BF16)
        self.ONES = sb("ONES", [128, 128], BF16)
        self.NEGONES = sb("NEGONES", [128, 128], BF16)
        self.SMALL = sb("SMALL", [128, 32], F32)
        self.WINF = [sb("WIN%d" % i, [128, KC * 256], BF16) for i in range(3)]
        self.WIN = [w[:, :].rearrange("p (k c) -> p k c", c=256) for w in self.WINF]
        self.WOUTF = [sb("WOUT%d" % i, [128, 6 * D], BF16) for i in range(2)]
        self.WOUT = [w[:, :].rearrange("p (j d) -> p j d", d=D) for w in self.WOUTF]
        NSCR = 36 * 1024
        self.SCR = sb("SCR", [128, NSCR], BF16)
        self.ar = Arena(self.SCR, NSCR)
        self.PS = [es.enter_context(nc.psum_tensor("ps%d" % i, [128, TW], F32)) for i in range(8)]

        self.bX = [[Buf("X%d_%d" % (k, t)) for t in range(NT)] for k in range(KC)]
        self.bXN = [[Buf("XN%d_%d" % (k, t)) for t in range(NT)] for k in range(KC)]
        self.bPS = [Buf("ps%d" % i) for i in range(8)]
        self.bWIN = [Buf("win%d" % i) for i in range(3)]
        self.bWOUT = [Buf("wout%d" % i) for i in range(2)]
        self.bVEC = Buf("vec")
        self.bCST = Buf("cst")
        self.bONES = Buf("ones")
        self.bSMALL = Buf("small")
        self.bOUT = Buf("out")
        self.win_i = 0
        self.wout_i = 0
        self.ps_i = 0

        self.build()

    def tsl(self, t):
        return slice(t * TW, (t + 1) * TW)

    def next_ps(self):
        i = self.ps_i
        self.ps_i = (self.ps_i + 1) % 8
        return i

    def load_win(self, src):
        i = self.win_i
        self.win_i = (self.win_i + 1) % 3
        pairs = [(self.WIN[i][:, 0:4, :], src[:, 0:4, :]), (self.WIN[i][:, 4:8, :], src[:, 4:8, :])]
        self.sc.dma(self.sc.pool, pairs, writes=[self.bWIN[i]])
        return i

    def load_wout(self, src, nch):
        i = self.wout_i
        self.wout_i = (self.wout_i + 1) % 2
        pairs = []
        j = 0
        while j < nch:
            e = min(j + 2, nch)
            pairs.append((self.WOUT[i][:, j:e, :], src[:, j:e, :]))
            j = e
        self.sc.dma(self.sc.pool, pairs, writes=[self.bWOUT[i]])
        return i

    def build(self):
        sc = self.sc
        for k in range(KC):
            sc.dma(sc.sp, [(self.X[:, k, :], self.dram["x"][:, k, :])], writes=self.bX[k])
        sc.dma(sc.sp, [(self.VEC[:, :], self.dram["vecs"][:, :])], writes=[self.bVEC])
        sc.dma(sc.pool, [(self.CST[:, :, :], self.dram["consts"][:, :, :])], writes=[self.bCST])
        sc.op(sc.dve, lambda e: e.memset(self.ONES[:, :], 1.0), writes=[self.bONES])
        sc.op(sc.dve, lambda e: e.memset(self.NEGONES[:, :], -1.0), writes=[self.bONES])

        for ph in self.phases:
            if ph[0] == "ffn":
                self.ffn(ph[1], ph[2])
            elif ph[0] == "attn":
                self.attn()
            elif ph[0] == "lru":
                self.lru()
            sc.barrier()

        for k in range(KC):
            sc.dma(sc.sp, [(self.y[:, k, :], self.X[:, k, :])], reads=self.bX[k], sem_buf=self.bOUT)
        sc.wait_tok(sc.sp, (self.bOUT.dsem, self.bOUT.dcnt))

        with self.nc.Block() as block:
            @block.tensor
            def _(e):
                Sched.replay(sc.pe, e)

            @block.scalar
            def _(e):
                Sched.replay(sc.act, e)

            @block.vector
            def _(e):
                Sched.replay(sc.dve, e)

            @block.gpsimd
            def _(e):
                Sched.replay(sc.pool, e)

            @block.sync
            def _(e):
                Sched.replay(sc.sp, e)
        self.es.close()

    def rmsnorm(self, gcol):
        sc = self.sc
        ar = self.ar
        SQ = [ar.alloc([TW], BF16) for _ in range(KC)]
        bSQ = [Buf("sq%d" % k) for k in range(KC)]
        TMP = [ar.alloc([TW], F32) for _ in range(2)]
        bTMP = [Buf("tmp%d" % i) for i in range(2)]
        RSTD = [ar.alloc([TW], F32) for _ in range(2)]
        bRSTD = [Buf("rstd%d" % i) for i in range(2)]
        for t in range(NT):
            ts = self.tsl(t)
            for k in range(KC):
                sc.op(sc.act, lambda e, k=k, ts=ts: e.activation(out=SQ[k], in_=self.X[:, k, ts], func=AF.Square),
                      reads=[self.bX[k][t]], writes=[bSQ[k]])
            pi = self.next_ps()

            def mm(e, pi=pi):
                for k in range(KC):
                    ins = e.matmul(self.PS[pi][:, :], self.ONES[:, :], SQ[k], start=(k == 0), stop=(k == KC - 1))
                return ins
            sc.op(sc.pe, mm, reads=bSQ + [self.bONES], writes=[self.bPS[pi]])
            i2 = t % 2
            sc.op(sc.act, lambda e, pi=pi, i2=i2: e.activation(out=TMP[i2], in_=self.PS[pi][:, :], func=AF.Ln,
                                                               scale=1.0 / D, bias=EPS),
                  reads=[self.bPS[pi]], writes=[bTMP[i2]])
            sc.op(sc.act, lambda e, i2=i2: e.activation(out=RSTD[i2], in_=TMP[i2], func=AF.Exp, scale=-0.5),
                  reads=[bTMP[i2]], writes=[bRSTD[i2]])
            for k in range(KC):
                sc.op(sc.dve, lambda e, k=k, ts=ts, i2=i2: e.scalar_tensor_tensor(
                    out=self.XN[:, k, ts], in0=self.X[:, k, ts], scalar=self.VEC[:, gcol + k:gcol + k + 1],
                    in1=RSTD[i2], op0=ALU.mult, op1=ALU.mult),
                    reads=[self.bX[k][t], bRSTD[i2], self.bVEC], writes=[self.bXN[k][t]])

    def ffn(self, layer, which):
        sc = self.sc
        ar = self.ar
        ar.reset()
        win = self.dram["l%d_%s_win" % (layer, which)]
        wout = self.dram["l%d_%s_wout" % (layer, which)]
        gcol = V_NORM["l%d_%s_norm" % (layer, which)]
        ACTB = ar.alloc([6, S], BF16)
        bACT = [[Buf("act%d_%d" % (j, t)) for t in range(NT)] for j in range(6)]
        SG = [ar.alloc([TW], F32) for _ in range(3)]
        bSG = [Buf("sg%d" % i) for i in range(3)]
        sgi = 0
        LOOK = 2
        slots = {}
        for j in range(LOOK):
            slots[j] = self.load_win(win[j])
        self.rmsnorm(gcol)
        for (j0, nj) in ((0, 6), (6, 6), (12, 5), (17, 5)):
            wo_slot = None
            for jj in range(nj):
                j = j0 + jj
                if jj == 1:
                    wo_slot = self.load_wout(wout[:, j0:j0 + nj, :], nj)
                if j + LOOK < NFC:
                    slots[j + LOOK] = self.load_win(win[j + LOOK])
                ws = slots[j]
                for t in range(NT):
                    ts = self.tsl(t)
                    pg = self.next_ps()
                    pu = self.next_ps()

                    def mm(e, ws=ws, ts=ts, pg=pg, pu=pu):
                        for k in range(KC):
                            e.matmul(self.PS[pg][:, :], self.WIN[ws][:, k, 0:128], self.XN[:, k, ts],
                                     start=(k == 0), stop=(k == KC - 1))
                        for k in range(KC):
                            ins = e.matmul(self.PS[pu][:, :], self.WIN[ws][:, k, 128:256], self.XN[:, k, ts],
                                           start=(k == 0), stop=(k == KC - 1))
                        return ins
                    sc.op(sc.pe, mm, reads=[self.bWIN[ws]] + [self.bXN[k][t] for k in range(KC)],
                          writes=[self.bPS[pg], self.bPS[pu]])
                    si = sgi
                    sgi = (sgi + 1) % 3
                    sc.op(sc.act, lambda e, pg=pg, si=si: e.activation(out=SG[si], in_=self.PS[pg][:, :], func=AF.Silu),
                          reads=[self.bPS[pg]], writes=[bSG[si]])
                    sc.op(sc.dve, lambda e, pu=pu, si=si, jj=jj, ts=ts: e.tensor_tensor(
                        out=ACTB[:, jj, ts], in0=SG[si], in1=self.PS[pu][:, :], op=ALU.mult),
                        reads=[bSG[si], self.bPS[pu]], writes=[bACT[jj][t]])
            for dm in range(KC):
                for t in range(NT):
                    ts = self.tsl(t)
                    po = self.next_ps()

                    def mm2(e, dm=dm, ts=ts, po=po, wo_slot=wo_slot, nj=nj):
                        for jj in range(nj):
                            ins = e.matmul(self.PS[po][:, :], self.WOUT[wo_slot][:, jj, dm * 128:(dm + 1) * 128],
                                           ACTB[:, jj, ts], start=(jj == 0), stop=(jj == nj - 1))
                        return ins
                    sc.op(sc.pe, mm2, reads=[self.bWOUT[wo_slot]] + [bACT[jj][t] for jj in range(nj)],
                          writes=[self.bPS[po]])
                    sc.op(sc.dve, lambda e, dm=dm, ts=ts, po=po: e.scalar_tensor_tensor(
                        out=self.X[:, dm, ts], in0=self.PS[po][:, :], scalar=0.5, in1=self.X[:, dm, ts],
                        op0=ALU.mult, op1=ALU.add),
                        reads=[self.bPS[po], self.bX[dm][t]], writes=[self.bX[dm][t]])

    def load_slot(self, kind, make_pairs):
        if kind == "win":
            i = self.win_i
            self.win_i = (self.win_i + 1) % 3
            self.sc.dma(self.sc.pool, make_pairs(self.WINF[i]), writes=[self.bWIN[i]])
        else:
            i = self.wout_i
            self.wout_i = (self.wout_i + 1) % 2
            self.sc.dma(self.sc.pool, make_pairs(self.WOUTF[i]), writes=[self.bWOUT[i]])
        return i

    def attn(self):
        sc = self.sc
        ar = self.ar
        ar.reset()
        wqkv = self.dram["wqkv"]
        wo = self.dram["wo_sb"]
        NEGTRI = self.CST[:, 0, :]
        MASK = self.CST[:, 1, :]
        BONES = self.CST[:, 2, :]

        def ld_qkv(hp):
            return self.load_slot("wout", lambda W: [
                (W[:, 0:3072].rearrange("p (k c) -> p k c", c=384)[:, 0:4, :], wqkv[hp][:, 0:4, :]),
                (W[:, 0:3072].rearrange("p (k c) -> p k c", c=384)[:, 4:8, :], wqkv[hp][:, 4:8, :])])

        def ld_wo(hp):
            return self.load_slot("win", lambda W: [(W[:, 0:D], wo[:, hp, :])])

        qkv_slot = {0: ld_qkv(0)}
        wo_slot = {0: ld_wo(0)}

        mark = ar.off
        self.rmsnorm(V_NORM["l0_mix_norm"])
        sc.barrier()
        ar.off = mark

        QA = [ar.alloc([S], BF16) for _ in range(2)]
        QB = [ar.alloc([S], BF16) for _ in range(2)]
        KT = [ar.alloc([S], BF16) for _ in range(2)]
        VV = [ar.alloc([16, 128], BF16) for _ in range(2)]
        OP = [ar.alloc([S], BF16) for _ in range(2)]
        bQ = [[Buf("q%d_%d" % (i, t)) for t in range(NT)] for i in range(2)]
        bK = [[Buf("k%d_%d" % (i, t)) for t in range(NT)] for i in range(2)]
        bV = [[Buf("v%d_%d" % (i, g)) for g in range(4)] for i in range(2)]
        bO = [[Buf("o%d_%d" % (i, t)) for t in range(NT)] for i in range(2)]
        SQ2 = [ar.alloc([TW], BF16) for _ in range(2)]
        bSQ2 = [Buf("sq2_%d" % i) for i in range(2)]
        TMP2 = [ar.alloc([TW], F32) for _ in range(2)]
        bTMP2 = [Buf("tmp2_%d" % i) for i in range(2)]
        RS2 = [ar.alloc([TW], F32) for _ in range(2)]
        bRS2 = [Buf("rs2_%d" % i) for i in range(2)]
        E = [ar.alloc([TW], F32) for _ in range(2)]
        bE = [Buf("e%d" % i) for i in range(2)]
        SP = [ar.alloc([TW], BF16) for _ in range(3)]
        bSP = [Buf("sp%d" % i) for i in range(3)]
        SS = [ar.alloc([TW], BF16) for _ in range(2)]
        bSS = [Buf("ss%d" % i) for i in range(2)]
        WW = [ar.alloc([TW], BF16) for _ in range(3)]
        bWW = [Buf("ww%d" % i) for i in range(3)]

        GQ = self.SMALL[:, 0:1]
        sc.op(sc.dve, lambda e: e.tensor_scalar(out=GQ, in0=self.VEC[:, V_QG:V_QG + 1], scalar1=0.125, scalar2=None,
                                                op0=ALU.mult), reads=[self.bVEC], writes=[self.bSMALL])
        GK = self.VEC[:, V_KG:V_KG + 1]
        for i in range(2):
            sc.op(sc.dve, lambda e, i=i: e.memset(QA[i][64:128, :], 0.0), writes=bQ[i])
            sc.op(sc.dve, lambda e, i=i: e.memset(QB[i][0:64, :], 0.0), writes=bQ[i])

        PZ = [0, 1]
        PC = [2, 3]
        POB = [4, 5]
        PJ = [6, 7]
        cnt = {"pj": 0, "n2": 0}

        def pj():
            cnt["pj"] += 1
            return PJ[cnt["pj"] % 2]

        def project(hp):
            st = hp % 2
            ws = qkv_slot[hp]
            WQ = self.WOUTF[ws][:, 0:3072].rearrange("p (k c) -> p k c", c=384)
            for t in range(NT):
                ts = self.tsl(t)
                for which in (0, 1):
                    pr = pj()

                    def mm(e, pr=pr, which=which, ts=ts):
                        for k in range(KC):
                            ins = e.matmul(self.PS[pr][:, :], WQ[:, k, which * 128:(which + 1) * 128], self.XN[:, k, ts],
                                           start=(k == 0), stop=(k == KC - 1))
                        return ins
                    sc.op(sc.pe, mm, reads=[self.bWOUT[ws]] + [self.bXN[k][t] for k in range(KC)], writes=[self.bPS[pr]])
                    n2 = cnt["n2"] % 2
                    cnt["n2"] += 1
                    sc.op(sc.act, lambda e, pr=pr, n2=n2: e.activation(out=SQ2[n2], in_=self.PS[pr][:, :], func=AF.Square),
                          reads=[self.bPS[pr]], writes=[bSQ2[n2]])
                    p2 = pj()
                    sc.op(sc.pe, lambda e, p2=p2, n2=n2: e.matmul(self.PS[p2][:, :], BONES, SQ2[n2], start=True, stop=True),
                          reads=[bSQ2[n2], self.bCST], writes=[self.bPS[p2]])
                    sc.op(sc.act, lambda e, p2=p2, n2=n2: e.activation(out=TMP2[n2], in_=self.PS[p2][:, :], func=AF.Ln,
                                                                       scale=1.0 / 64, bias=EPS),
                          reads=[self.bPS[p2]], writes=[bTMP2[n2]])
                    sc.op(sc.act, lambda e, n2=n2: e.activation(out=RS2[n2], in_=TMP2[n2], func=AF.Exp, scale=-0.5),
                          reads=[bTMP2[n2]], writes=[bRS2[n2]])
                    if which == 0:
                        sc.op(sc.dve, lambda e, pr=pr, n2=n2, ts=ts: e.scalar_tensor_tensor(
                            out=QA[st][0:64, ts], in0=self.PS[pr][0:64, :], scalar=GQ[0:64, :], in1=RS2[n2][0:64, :],
                            op0=ALU.mult, op1=ALU.mult), reads=[self.bPS[pr], bRS2[n2], self.bSMALL], writes=[bQ[st][t]])
                        sc.op(sc.dve, lambda e, pr=pr, n2=n2, ts=ts: e.scalar_tensor_tensor(
                            out=QB[st][64:128, ts], in0=self.PS[pr][64:128, :], scalar=GQ[64:128, :], in1=RS2[n2][64:128, :],
                            op0=ALU.mult, op1=ALU.mult), reads=[self.bPS[pr], bRS2[n2], self.bSMALL], writes=[bQ[st][t]])
                    else:
                        sc.op(sc.dve, lambda e, pr=pr, n2=n2, ts=ts: e.scalar_tensor_tensor(
                            out=KT[st][:, ts], in0=self.PS[pr][:, :], scalar=GK, in1=RS2[n2],
                            op0=ALU.mult, op1=ALU.mult), reads=[self.bPS[pr], bRS2[n2], self.bVEC], writes=[bK[st][t]])
            for g in range(4):
                pr = pj()

                def mmv(e, pr=pr, g=g):
                    for q4 in range(4):
                        scn = g * 4 + q4
                        for k in range(KC):
                            ins = e.matmul(self.PS[pr][:, q4 * 128:(q4 + 1) * 128], self.XN[:, k, scn * 128:(scn + 1) * 128],
                                           WQ[:, k, 256:384], start=(k == 0), stop=(k == KC - 1))
                    return ins
                sc.op(sc.pe, mmv, reads=[self.bWOUT[ws]] + [self.bXN[k][g] for k in range(KC)], writes=[self.bPS[pr]])
                sc.op(sc.act, lambda e, pr=pr, g=g: e.activation(
                    out=VV[st][:, g * 4:(g + 1) * 4, :].rearrange("p a b -> p (a b)"), in_=self.PS[pr][:, :], func=AF.Copy),
                    reads=[self.bPS[pr]], writes=[bV[st][g]])

        def core(hp):
            st = hp % 2
            items = []
            for hl in range(2):
                for t in range(NT):
                    for c in range(4 * t + 3, -1, -1):
                        items.append((hl, t, c))
            n = len(items)
            state = {}

            def stageA(i):
                hl, t, c = items[i]
                lo = max(0, 128 * (c - 4 * t))
                diag = c >= 4 * t
                first = (c == 4 * t + 3)
                gi = (hl * NT + t) % 2
                Qp = QA[st] if hl == 0 else QB[st]
                qs = slice(t * TW + lo, (t + 1) * TW)
                pz = PZ[i % 2]
                ei = i % 2
                si = i % 3
                if first:
                    sc.op(sc.dve, lambda e, gi=gi: e.memset(SS[gi], 0.0), writes=[bSS[gi]])
                sc.op(sc.pe, lambda e, pz=pz, c=c, lo=lo, qs=qs, Qp=Qp: e.matmul(
                    self.PS[pz][:, lo:TW], KT[st][:, c * 128:(c + 1) * 128], Qp[:, qs], start=True, stop=True),
                    reads=[bK[st][c // 4], bQ[st][t]], writes=[self.bPS[pz]])
                sc.op(sc.act, lambda e, pz=pz, ei=ei, lo=lo: e.activation(out=E[ei][:, lo:TW], in_=self.PS[pz][:, lo:TW],
                                                                          func=AF.Exp),
                      reads=[self.bPS[pz]], writes=[bE[ei]])
                sc.op(sc.act, lambda e, ei=ei, si=si, lo=lo: e.activation(out=SP[si][:, lo:TW], in_=E[ei][:, lo:TW],
                                                                          func=AF.Ln, bias=1.0),
                      reads=[bE[ei]], writes=[bSP[si]])
                if diag:
                    sc.op(sc.dve, lambda e, si=si, lo=lo: e.tensor_tensor(out=SP[si][:, lo:lo + 128], in0=SP[si][:, lo:lo + 128],
                                                                          in1=MASK, op=ALU.mult),
                          reads=[bSP[si], self.bCST], writes=[bSP[si]])

            def stageB(i):
                hl, t, c = items[i]
                lo = max(0, 128 * (c - 4 * t))
                diag = c >= 4 * t
                first = (c == 4 * t + 3)
                gi = (hl * NT + t) % 2
                Qp = QA[st] if hl == 0 else QB[st]
                qs = slice(t * TW + lo, (t + 1) * TW)
                pc = PC[i % 2]
                si = i % 3
                wi = i % 3

                def mm(e):
                    e.matmul(self.PS[pc][:, lo:TW], NEGTRI, SP[si][:, lo:TW], start=True, stop=False)
                    if not first:
                        e.matmul(self.PS[pc][:, lo:TW], self.NEGONES[:, :], SS[gi][:, lo:TW], start=False, stop=False)
                    return e.matmul(self.PS[pc][:, lo:TW], KT[st][:, c * 128:(c + 1) * 128], Qp[:, qs], start=False, stop=True)
                sc.op(sc.pe, mm, reads=[bSP[si], bSS[gi], self.bCST, self.bONES, bK[st][c // 4], bQ[st][t]],
                      writes=[self.bPS[pc]])
                sc.op(sc.act, lambda e: e.activation(out=WW[wi][:, lo:TW], in_=self.PS[pc][:, lo:TW], func=AF.Exp),
                      reads=[self.bPS[pc]], writes=[bWW[wi]])
                if diag:
                    sc.op(sc.dve, lambda e: e.tensor_tensor(out=WW[wi][:, lo:lo + 128], in0=WW[wi][:, lo:lo + 128],
                                                            in1=MASK, op=ALU.mult),
                          reads=[bWW[wi], self.bCST], writes=[bWW[wi]])
                if c > 0:
                    sc.op(sc.dve, lambda e: e.tensor_tensor(out=SS[gi][:, lo:TW], in0=SS[gi][:, lo:TW], in1=SP[si][:, lo:TW],
                                                            op=ALU.add),
                          reads=[bSS[gi], bSP[si]], writes=[bSS[gi]])

            def stageC(i):
                hl, t, c = items[i]
                lo = max(0, 128 * (c - 4 * t))
                first = (c == 4 * t + 3)
                gi = (hl * NT + t) % 2
                po = POB[gi]
                wi = i % 3
                sc.op(sc.pe, lambda e: e.matmul(self.PS[po][:, lo:TW], VV[st][:, c, :], WW[wi][:, lo:TW],
                                                start=first, stop=(c == 0)),
                      reads=[bV[st][c // 4], bWW[wi]], writes=[self.bPS[po]])
                if c == 0:
                    rows = slice(hl * 64, (hl + 1) * 64)
                    sc.op(sc.dve, lambda e: e.tensor_copy(out=OP[st][rows, self.tsl(t)], in_=self.PS[po][rows, :]),
                          reads=[self.bPS[po]], writes=[bO[st][t]])

            for step in range(n + 2):
                if step < n:
                    stageA(step)
                if 0 <= step - 1 < n:
                    stageB(step - 1)
                if 0 <= step - 2 < n:
                    stageC(step - 2)

        def oproj(hp):
            st = hp % 2
            ws = wo_slot[hp]
            WO = self.WINF[ws]
            for dm in range(KC):
                for t in range(NT):
                    pr = pj()
                    sc.op(sc.pe, lambda e, pr=pr, dm=dm, t=t: e.matmul(self.PS[pr][:, :], WO[:, dm * 128:(dm + 1) * 128],
                                                                      OP[st][:, self.tsl(t)], start=True, stop=True),
                          reads=[self.bWIN[ws], bO[st][t]], writes=[self.bPS[pr]])
                    sc.op(sc.dve, lambda e, pr=pr, dm=dm, t=t: e.tensor_tensor(
                        out=self.X[:, dm, self.tsl(t)], in0=self.PS[pr][:, :], in1=self.X[:, dm, self.tsl(t)], op=ALU.add),
                        reads=[self.bPS[pr], self.bX[dm][t]], writes=[self.bX[dm][t]])

        for hp in range(8):
            if hp + 1 < 8:
                qkv_slot[hp + 1] = ld_qkv(hp + 1)
                wo_slot[hp + 1] = ld_wo(hp + 1)
            project(hp)
            core(hp)
            oproj(hp)

    def lru(self):
        sc = self.sc
        ar = self.ar
        ar.reset()
        win = self.dram["lru_win"]
        wo = self.dram["lru_wo"]
        slots = {0: self.load_win(win[0]), 1: self.load_win(win[1])}
        mark = ar.off
        self.rmsnorm(V_NORM["l1_mix_norm"])
        sc.barrier()
        ar.off = mark
        wos = {}
        wos[0] = self.load_slot("wout", lambda W: [(W[:, 0:4 * D].rearrange("p (j d) -> p j d", d=D), wo[:, 0:4, :])])
        wos[1] = self.load_slot("wout", lambda W: [(W[:, 0:4 * D].rearrange("p (j d) -> p j d", d=D), wo[:, 4:8, :])])
        WR = self.WOUTF[wos[0]][:, 4096:5120].rearrange("p (a b) -> p a b", b=128)
        WI = self.WOUTF[wos[0]][:, 5120:6144].rearrange("p (a b) -> p a b", b=128)
        bWG = Buf("wgate")
        sc.op(sc.dve, lambda e: e.memset(WR, 0.0), writes=[bWG])
        sc.op(sc.dve, lambda e: e.memset(WI, 0.0), writes=[bWG])
        pairs = []
        for (Wt, nm) in ((WR, "lru_wr"), (WI, "lru_wi")):
            src = self.dram[nm]
            pairs.append((Wt[0:64, :, 0:64], src[0:64, :, :]))
            pairs.append((Wt[64:128, :, 64:128], src[64:128, :, :]))
        sc.dma(sc.pool, pairs, writes=[bWG])
        C1 = self.SMALL[:, 8:16]
        C2 = self.SMALL[:, 16:24]
        TS = self.SMALL[:, 24:32]
        sc.op(sc.act, lambda e: e.activation(out=TS, in_=self.VEC[:, V_LAM:V_LAM + 8], func=AF.Exp, scale=-1.0),
              reads=[self.bVEC], writes=[self.bSMALL])
        sc.op(sc.act, lambda e: e.activation(out=TS, in_=TS, func=AF.Ln, bias=1.0), reads=[self.bSMALL], writes=[self.bSMALL])
        sc.op(sc.dve, lambda e: e.tensor_scalar(out=C1, in0=TS, scalar1=-8.0, scalar2=None, op0=ALU.mult),
              reads=[self.bSMALL], writes=[self.bSMALL])
        sc.op(sc.dve, lambda e: e.tensor_scalar(out=C2, in0=TS, scalar1=-16.0, scalar2=None, op0=ALU.mult),
              reads=[self.bSMALL], writes=[self.bSMALL])

        PAD = 4
        B1 = ar.alloc([S + PAD], F32)
        B2 = ar.alloc([S], F32)
        B3 = ar.alloc([S], F32)
        B4 = ar.alloc([S], F32)
        b1, b2, b3, b4 = Buf("B1"), Buf("B2"), Buf("B3"), Buf("B4")
        HY = ar.alloc([KC, S], BF16)
        bHY = [Buf("hy%d" % j) for j in range(KC)]
        XCB = [ar.alloc([TW], BF16) for _ in range(2)]
        bXCB = [Buf("xcb%d" % i) for i in range(2)]
        IT = [ar.alloc([TW], F32) for _ in range(2)]
        bIT = [Buf("it%d" % i) for i in range(2)]
        sc.op(sc.dve, lambda e: e.memset(B1[:, 0:PAD], 0.0), writes=[b1])
        XB = B1[:, PAD:PAD + S]
        for j in range(KC):
            if j + 2 < KC:
                slots[j + 2] = self.load_win(win[j + 2])
            ws = slots[j]
            for t in range(NT):
                ts = self.tsl(t)
                px = self.next_ps()
                py = self.next_ps()

                def mm(e, ws=ws, ts=ts, px=px, py=py):
                    for k in range(KC):
                        e.matmul(self.PS[px][:, :], self.WIN[ws][:, k, 0:128], self.XN[:, k, ts], start=(k == 0), stop=(k == KC - 1))
                    for k in range(KC):
                        ins = e.matmul(self.PS[py][:, :], self.WIN[ws][:, k, 128:256], self.XN[:, k, ts], start=(k == 0),
                                       stop=(k == KC - 1))
                    return ins
                sc.op(sc.pe, mm, reads=[self.bWIN[ws]] + [self.bXN[k][t] for k in range(KC)], writes=[self.bPS[px], self.bPS[py]])
                sc.op(sc.act, lambda e, px=px, ts=ts: e.activation(out=XB[:, ts], in_=self.PS[px][:, :], func=AF.Copy),
                      reads=[self.bPS[px]], writes=[b1])
                sc.op(sc.act, lambda e, py=py, ts=ts: e.activation(out=B2[:, ts], in_=self.PS[py][:, :], func=AF.Copy),
                      reads=[self.bPS[py]], writes=[b2])
            sc.op(sc.dve, lambda e: e.tensor_tensor(out=B4, in0=B2, in1=B2, op=ALU.mult), reads=[b2], writes=[b4])
            sc.op(sc.dve, lambda e: e.tensor_scalar(out=B4, in0=B4, scalar1=0.044715, scalar2=1.0, op0=ALU.mult, op1=ALU.add),
                  reads=[b4], writes=[b4])
            sc.op(sc.dve, lambda e: e.tensor_tensor(out=B4, in0=B4, in1=B2, op=ALU.mult), reads=[b4, b2], writes=[b4])
            sc.op(sc.act, lambda e: e.activation(out=B4, in_=B4, func=AF.Sigmoid, scale=1.5957691216057308),
                  reads=[b4], writes=[b4])
            sc.op(sc.dve, lambda e: e.tensor_tensor(out=B2, in0=B2, in1=B4, op=ALU.mult), reads=[b4, b2], writes=[b2])
            cw = lambda tap, j=j: self.VEC[:, V_CONVW + tap * 8 + j: V_CONVW + tap * 8 + j + 1]
            cb = self.VEC[:, V_CONVB + j:V_CONVB + j + 1]
            sc.op(sc.dve, lambda e, cw=cw, cb=cb: e.tensor_scalar(out=B3, in0=B1[:, 1:1 + S], scalar1=cw(0), scalar2=cb,
                                                                  op0=ALU.mult, op1=ALU.add),
                  reads=[b1, self.bVEC], writes=[b3])
            for tap in (1, 2, 3):
                sc.op(sc.dve, lambda e, cw=cw, tap=tap: e.scalar_tensor_tensor(
                    out=B3, in0=B1[:, 1 + tap:1 + tap + S], scalar=cw(tap), in1=B3, op0=ALU.mult, op1=ALU.add),
                    reads=[b1, b3, self.bVEC], writes=[b3])
            for t in range(NT):
                ts = self.tsl(t)
                xi = t % 2
                sc.op(sc.act, lambda e, xi=xi, ts=ts: e.activation(out=XCB[xi], in_=B3[:, ts], func=AF.Copy),
                      reads=[b3], writes=[bXCB[xi]])
                pr = self.next_ps()
                pi = self.next_ps()
                sc.op(sc.pe, lambda e, pr=pr, xi=xi, j=j: e.matmul(self.PS[pr][:, :], WR[:, j, :], XCB[xi], start=True, stop=True),
                      reads=[bWG, bXCB[xi]], writes=[self.bPS[pr]])
                sc.op(sc.pe, lambda e, pi=pi, xi=xi, j=j: e.matmul(self.PS[pi][:, :], WI[:, j, :], XCB[xi], start=True, stop=True),
                      reads=[bWG, bXCB[xi]], writes=[self.bPS[pi]])
                sc.op(sc.act, lambda e, pr=pr, ts=ts, j=j: e.activation(out=XB[:, ts], in_=self.PS[pr][:, :], func=AF.Sigmoid,
                                                                        bias=self.VEC[:, V_BR + j:V_BR + j + 1]),
                      reads=[self.bPS[pr], b3, self.bVEC], writes=[b1])
                sc.op(sc.act, lambda e, pi=pi, xi=xi, j=j: e.activation(out=IT[xi], in_=self.PS[pi][:, :], func=AF.Sigmoid,
                                                                        bias=self.VEC[:, V_BI + j:V_BI + j + 1]),
                      reads=[self.bPS[pi], self.bVEC], writes=[bIT[xi]])
                sc.op(sc.dve, lambda e, xi=xi, ts=ts: e.tensor_tensor(out=B3[:, ts], in0=B3[:, ts], in1=IT[xi], op=ALU.mult),
                      reads=[b3, bIT[xi], bXCB[xi]], writes=[b3])
            sc.op(sc.act, lambda e, j=j: e.activation(out=B4, in_=XB, func=AF.Exp, scale=C2[:, j:j + 1]),
                  reads=[b1, self.bSMALL, b2], writes=[b4])
            sc.op(sc.act, lambda e, j=j: e.activation(out=XB, in_=XB, func=AF.Exp, scale=C1[:, j:j + 1]),
                  reads=[b1, self.bSMALL], writes=[b1])
            sc.op(sc.act, lambda e: e.activation(out=B4, in_=B4, func=AF.Sqrt, scale=-1.0, bias=1.0), reads=[b4], writes=[b4])
            sc.op(sc.dve, lambda e: e.tensor_tensor(out=B3, in0=B3, in1=B4, op=ALU.mult), reads=[b3, b4], writes=[b3])
            sc.op(sc.dve, lambda e: e.tensor_tensor_scan(out=B4, data0=XB, data1=B3, initial=0.0, op0=ALU.mult, op1=ALU.add),
                  reads=[b1, b3], writes=[b4])
            sc.op(sc.dve, lambda e, j=j: e.tensor_tensor(out=HY[:, j, :], in0=B4, in1=B2, op=ALU.mult),
                  reads=[b4, b2], writes=[bHY[j]])
        for dm in range(KC):
            for t in range(NT):
                ts = self.tsl(t)
                po = self.next_ps()

                def mm2(e, dm=dm, ts=ts, po=po):
                    for j in range(KC):
                        W = self.WOUTF[wos[j // 4]][:, 0:4 * D].rearrange("p (j d) -> p j d", d=D)
                        ins = e.matmul(self.PS[po][:, :], W[:, j % 4, dm * 128:(dm + 1) * 128], HY[:, j, ts],
                                       start=(j == 0), stop=(j == KC - 1))
                    return ins
                sc.op(sc.pe, mm2, reads=[self.bWOUT[wos[0]], self.bWOUT[wos[1]]] + bHY, writes=[self.bPS[po]])
                sc.op(sc.dve, lambda e, dm=dm, ts=ts, po=po: e.tensor_tensor(
                    out=self.X[:, dm, ts], in0=self.PS[po][:, :], in1=self.X[:, dm, ts], op=ALU.add),
                    reads=[self.bPS[po], self.bX[dm][t]], writes=[self.bX[dm][t]])


def _chunked(w):
    c = w.shape[1]
    return np.ascontiguousarray(w.reshape(KC, 128, c).transpose(1, 0, 2))


def _win_slabs(w, nslab, half_off):
    a = _chunked(w)
    out = np.empty((nslab, 128, KC, 256), np.float32)
    for j in range(nslab):
        out[j, :, :, 0:128] = a[:, :, j * 128:(j + 1) * 128]
        out[j, :, :, 128:256] = a[:, :, half_off + j * 128: half_off + (j + 1) * 128]
    return out


def _vec_cols(v):
    return np.ascontiguousarray(v.reshape(KC, 128).T)


def prepare_shared(inp):
    sh = {}
    vecs = np.zeros((128, NV), np.float32)
    for name, col in V_NORM.items():
        vecs[:, col:col + 8] = _vec_cols(inp[name])
    for j in range(4):
        vecs[:, V_CONVW + j * 8: V_CONVW + (j + 1) * 8] = _vec_cols(inp["l1_lru_conv_w"][j])
    vecs[:, V_CONVB:V_CONVB + 8] = _vec_cols(inp["l1_lru_conv_b"])
    vecs[:, V_BR:V_BR + 8] = _vec_cols(inp["l1_lru_b_r"])
    vecs[:, V_BI:V_BI + 8] = _vec_cols(inp["l1_lru_b_i"])
    vecs[:, V_LAM:V_LAM + 8] = _vec_cols(inp["l1_lru_lambda"])
    vecs[:, V_QG] = np.tile(inp["l0_sb_q_norm"], 2)
    vecs[:, V_KG] = np.tile(inp["l0_sb_k_norm"], 2)
    sh["vecs"] = vecs
    j = np.arange(128)[:, None]
    s = np.arange(128)[None, :]
    consts = np.zeros((128, 3, 128), np.float32)
    consts[:, 0, :] = -(j >= s).astype(np.float32)
    consts[:, 1, :] = (s > j).astype(np.float32)
    consts[:, 2, :] = ((j // 64) == (s // 64)).astype(np.float32)
    sh["consts"] = consts
    for l in (0, 1):
        for f in ("ff1", "ff2"):
            sh["l%d_%s_win" % (l, f)] = _win_slabs(inp["l%d_%s_w_in" % (l, f)], NFC, FF)
            wo = inp["l%d_%s_w_out" % (l, f)]
            sh["l%d_%s_wout" % (l, f)] = np.ascontiguousarray(wo.reshape(NFC, 128, D).transpose(1, 0, 2))
    a = _chunked(inp["l0_sb_w_qkv"])
    wqkv = np.empty((8, 128, KC, 384), np.float32)
    for hp in range(8):
        for w3 in range(3):
            wqkv[hp, :, :, w3 * 128:(w3 + 1) * 128] = a[:, :, w3 * 1024 + hp * 128: w3 * 1024 + (hp + 1) * 128]
    sh["wqkv"] = wqkv
    sh["wo_sb"] = _chunked(inp["l0_sb_w_o"])
    sh["lru_win"] = _win_slabs(inp["l1_lru_w_in"], 8, 1024)
    for nm, key in (("lru_wr", "l1_lru_w_r"), ("lru_wi", "l1_lru_w_i")):
        w = inp[key]
        o = np.empty((128, 8, 64), np.float32)
        for jj in range(8):
            o[0:64, jj, :] = w[2 * jj]
            o[64:128, jj, :] = w[2 * jj + 1]
        sh[nm] = o
    sh["lru_wo"] = _chunked(inp["l1_lru_w_o"])
    return sh


def x_to_dev(xb):
    return np.ascontiguousarray(xb.T.reshape(KC, 128, S).transpose(1, 0, 2))


def y_from_dev(y):
    return np.ascontiguousarray(y.transpose(1, 0, 2).reshape(D, S).T)


FULL_PHASES = [("ffn", 0, "ff1"), ("attn",), ("ffn", 0, "ff2"),
               ("ffn", 1, "ff1"), ("lru",), ("ffn", 1, "ff2")]

_PROG_CACHE = {}
FUSED = False


def run_phases(inp, phases, xs, ncores):
    key = tuple(phases)
    if key not in _PROG_CACHE:
        _PROG_CACHE[key] = Prog(phases)
    prog = _PROG_CACHE[key]
    sh = prepare_shared(inp)
    in_maps = []
    for c in range(ncores):
        m = dict(sh)
        m["x"] = x_to_dev(xs[c])
        in_maps.append(m)
    res = run_bass_kernel_spmd(prog.nc, in_maps, core_ids=list(range(ncores)))
    return [y_from_dev(np.asarray(r["y"])) for r in res.results]


def kernel(**inputs):
    inp = {k: np.asarray(v, dtype=np.float32) for k, v in inputs.items()}
    x = inp["x"]
    if FUSED:
        outs = run_phases(inp, FULL_PHASES, [x[b] for b in range(8)], 8)
    else:
        mid = run_phases(inp, FULL_PHASES[:3], [x[b] for b in range(8)], 8)
        outs = run_phases(inp, FULL_PHASES[3:], mid, 8)
    return np.stack(outs, axis=0).astype(np.float32)
```

```python
import numpy as np
from contextlib import ExitStack
import concourse.bass as bass
import concourse.mybir as mybir
from concourse.bass_utils import run_bass_kernel_spmd

F32 = mybir.dt.float32
BF16 = mybir.dt.bfloat16
AF = mybir.ActivationFunctionType
ALU = mybir.AluOpType

S = 2048
D = 1024
KC = 8
NT = 4
TW = 512
FF = 2816
NFC = 22
EPS = 1e-6

V_NORM = {"l0_ff1_norm": 0, "l0_mix_norm": 8, "l0_ff2_norm": 16,
          "l1_ff1_norm": 24, "l1_mix_norm": 32, "l1_ff2_norm": 40}
V_CONVW = 48
V_CONVB = 80
V_BR = 88
V_BI = 96
V_LAM = 104
V_QG = 112
V_KG = 113
NV = 114


class Buf:
    __slots__ = ("name", "w", "r", "dsem", "dcnt")

    def __init__(self, name):
        self.name = name
        self.w = None
        self.r = {}
        self.dsem = None
        self.dcnt = 0


class Eng:
    def __init__(self, name, sem):
        self.name = name
        self.sem = sem
        self.cnt = 0
        self.prog = []
        self.waited = {}


class Sched:
    def __init__(self, nc, es):
        self.nc = nc
        self.es = es
        self.nsem = 0
        self.pe = Eng("pe", self.new_sem("pe"))
        self.act = Eng("act", self.new_sem("act"))
        self.dve = Eng("dve", self.new_sem("dve"))
        self.pool = Eng("pool", self.new_sem("pool"))
        self.sp = Eng("sp", self.new_sem("sp"))
        self.engs = [self.pe, self.act, self.dve, self.pool, self.sp]

    def new_sem(self, name="s"):
        self.nsem += 1
        return self.es.enter_context(self.nc.semaphore("%s_%d" % (name, self.nsem)))

    def _deps(self, eng, reads, writes):
        toks = []
        for b in reads:
            if b.w is not None:
                toks.append(b.w)
        for b in writes:
            if b.w is not None:
                toks.append(b.w)
            toks.extend(b.r.values())
        for (sem, val) in toks:
            key = id(sem)
            if eng.waited.get(key, 0) >= val:
                continue
            eng.waited[key] = val
            eng.prog.append(("wait", sem, val))

    @staticmethod
    def _mark(tok, reads, writes):
        key = id(tok[0])
        for b in reads:
            old = b.r.get(key)
            if old is None or old[1] < tok[1]:
                b.r[key] = tok
        for b in writes:
            b.w = tok
            b.r = {}

    def op(self, eng, fn, reads=(), writes=()):
        self._deps(eng, reads, writes)
        eng.cnt += 1
        tok = (eng.sem, eng.cnt)
        eng.prog.append(("op", fn, eng.sem))
        self._mark(tok, reads, writes)
        return tok

    def dma(self, eng, pairs, reads=(), writes=(), sem_buf=None):
        self._deps(eng, reads, writes)
        b = sem_buf if sem_buf is not None else writes[0]
        if b.dsem is None:
            b.dsem = self.new_sem("d")
        for (o, i) in pairs:
            b.dcnt += 16
            eng.prog.append(("dma", o, i, b.dsem))
        tok = (b.dsem, b.dcnt)
        self._mark(tok, reads, writes)
        return tok

    def wait_tok(self, eng, tok):
        key = id(tok[0])
        if eng.waited.get(key, 0) >= tok[1]:
            return
        eng.waited[key] = tok[1]
        eng.prog.append(("wait", tok[0], tok[1]))

    def barrier(self, engs=None):
        engs = engs or [self.pe, self.act, self.dve]
        toks = [(e.sem, e.cnt) for e in engs if e.cnt > 0]
        for e in engs:
            for t in toks:
                if t[0] is e.sem:
                    continue
                self.wait_tok(e, t)

    @staticmethod
    def replay(eng, e):
        for item in eng.prog:
            if item[0] == "wait":
                e.wait_ge(item[1], item[2])
            elif item[0] == "op":
                ins = item[1](e)
                ins.then_inc(item[2], 1)
            else:
                e.dma_start(out=item[1], in_=item[2]).then_inc(item[3], 16)


class Arena:
    def __init__(self, t, nelem):
        self.t = t
        self.n = nelem
        self.off = 0

    def reset(self):
        self.off = 0

    def alloc(self, shape, dtype):
        n = 1
        for s in shape:
            n *= s
        ne = n * (2 if dtype == F32 else 1)
        ne = (ne + 31) // 32 * 32
        assert self.off + ne <= self.n, ("arena overflow", self.off, ne, self.n)
        ap = self.t[:, self.off:self.off + ne]
        self.off += ne
        if dtype == F32:
            ap = ap.bitcast(F32)
        ap = ap[:, 0:n]
        if len(shape) == 2:
            ap = ap.rearrange("p (a b) -> p a b", b=shape[1])
        elif len(shape) == 3:
            ap = ap.rearrange("p (a b c) -> p a b c", b=shape[1], c=shape[2])
        return ap


class Prog:
    def __init__(self, phases):
        self.phases = phases
        nc = self.nc = bass.Bass("TRN2", target_bir_lowering=False)
        es = self.es = ExitStack()
        self.sc = Sched(nc, es)
        d = self.dram = {}

        def din(name, shape):
            d[name] = nc.dram_tensor(name, list(shape), F32, kind="ExternalInput").ap()

        din("x", [128, KC, S])
        din("vecs", [128, NV])
        din("consts", [128, 3, 128])
        for l in (0, 1):
            for f in ("ff1", "ff2"):
                din("l%d_%s_win" % (l, f), [NFC, 128, KC, 256])
                din("l%d_%s_wout" % (l, f), [128, NFC, D])
        din("wqkv", [8, 128, KC, 384])
        din("wo_sb", [128, KC, D])
        din("lru_win", [8, 128, KC, 256])
        din("lru_wr", [128, 8, 64])
        din("lru_wi", [128, 8, 64])
        din("lru_wo", [128, KC, D])
        self.y = nc.dram_tensor("y", [128, KC, S], F32, kind="ExternalOutput").ap()

        sb = lambda name, shape, dt: es.enter_context(nc.sbuf_tensor(name, shape, dt))
        self.X = sb("X", [128, KC, S], F32)
        self.XN = sb("XN", [128, KC, S], BF16)
        self.VEC = sb("VEC", [128, NV], F32)
        self.CST = sb("CST", [128, 3, 128], BF16)
        self.ONES = sb("ONES", [128, 128], BF16)
        self.NEGONES = sb("NEGONES", [128, 128], BF16)
        self.SMALL = sb("SMALL", [128, 32], F32)
        self.WINF = [sb("WIN%d" % i, [128, KC * 256], BF16) for i in range(3)]
        self.WIN = [w[:, :].rearrange("p (k c) -> p k c", c=256) for w in self.WINF]
        self.WOUTF = [sb("WOUT%d" % i, [128, 6 * D], BF16) for i in range(2)]
        self.WOUT = [w[:, :].rearrange("p (j d) -> p j d", d=D) for w in self.WOUTF]
        NSCR = 36 * 1024
        self.SCR = sb("SCR", [128, NSCR], BF16)
        self.ar = Arena(self.SCR, NSCR)
        self.PS = [es.enter_context(nc.psum_tensor("ps%d" % i, [128, TW], F32)) for i in range(8)]

        self.bX = [[Buf("X%d_%d" % (k, t)) for t in range(NT)] for k in range(KC)]
        self.bXN = [[Buf("XN%d_%d" % (k, t)) for t in range(NT)] for k in range(KC)]
        self.bPS = [Buf("ps%d" % i) for i in range(8)]
        self.bWIN = [Buf("win%d" % i) for i in range(3)]
        self.bWOUT = [Buf("wout%d" % i) for i in range(2)]
        self.bVEC = Buf("vec")
        self.bCST = Buf("cst")
        self.bONES = Buf("ones")
        self.bSMALL = Buf("small")
        self.bOUT = Buf("out")
        self.win_i = 0
        self.wout_i = 0
        self.ps_i = 0

        self.build()

    def tsl(self, t):
        return slice(t * TW, (t + 1) * TW)

    def next_ps(self):
        i = self.ps_i
        self.ps_i = (self.ps_i + 1) % 8
        return i

    def load_win(self, src):
        i = self.win_i
        self.win_i = (self.win_i + 1) % 3
        pairs = [(self.WIN[i][:, :, :], src[:, :, :])]
        self.sc.dma(self.sc.pool, pairs, writes=[self.bWIN[i]])
        return i

    def load_wout(self, src, nch):
        i = self.wout_i
        self.wout_i = (self.wout_i + 1) % 2
        pairs = [(self.WOUT[i][:, 0:nch, :], src[:, 0:nch, :])]
        self.sc.dma(self.sc.pool, pairs, writes=[self.bWOUT[i]])
        return i

    def build(self):
        sc = self.sc
        for k in range(KC):
            sc.dma(sc.sp, [(self.X[:, k, :], self.dram["x"][:, k, :])], writes=self.bX[k])
        sc.dma(sc.sp, [(self.VEC[:, :], self.dram["vecs"][:, :])], writes=[self.bVEC])
        sc.dma(sc.pool, [(self.CST[:, :, :], self.dram["consts"][:, :, :])], writes=[self.bCST])
        sc.op(sc.dve, lambda e: e.memset(self.ONES[:, :], 1.0), writes=[self.bONES])
        sc.op(sc.dve, lambda e: e.memset(self.NEGONES[:, :], -1.0), writes=[self.bONES])

        for ph in self.phases:
            if ph[0] == "ffn":
                self.ffn(ph[1], ph[2])
            elif ph[0] == "attn":
                self.attn()
            elif ph[0] == "lru":
                self.lru()
            sc.barrier()

        for k in range(KC):
            sc.dma(sc.sp, [(self.y[:, k, :], self.X[:, k, :])], reads=self.bX[k], sem_buf=self.bOUT)
        sc.wait_tok(sc.sp, (self.bOUT.dsem, self.bOUT.dcnt))

        with self.nc.Block() as block:
            @block.tensor
            def _(e):
                Sched.replay(sc.pe, e)

            @block.scalar
            def _(e):
                Sched.replay(sc.act, e)

            @block.vector
            def _(e):
                Sched.replay(sc.dve, e)

            @block.gpsimd
            def _(e):
                Sched.replay(sc.pool, e)

            @block.sync
            def _(e):
                Sched.replay(sc.sp, e)
        self.es.close()

    def rmsnorm(self, gcol):
        sc = self.sc
        ar = self.ar
        SQ = [ar.alloc([TW], BF16) for _ in range(KC)]
        bSQ = [Buf("sq%d" % k) for k in range(KC)]
        TMP = [ar.alloc([TW], F32) for _ in range(2)]
        bTMP = [Buf("tmp%d" % i) for i in range(2)]
        RSTD = [ar.alloc([TW], F32) for _ in range(2)]
        bRSTD = [Buf("rstd%d" % i) for i in range(2)]
        for t in range(NT):
            ts = self.tsl(t)
            for k in range(KC):
                sc.op(sc.act, lambda e, k=k, ts=ts: e.activation(out=SQ[k], in_=self.X[:, k, ts], func=AF.Square),
                      reads=[self.bX[k][t]], writes=[bSQ[k]])
            pi = self.next_ps()

            def mm(e, pi=pi):
                for k in range(KC):
                    ins = e.matmul(self.PS[pi][:, :], self.ONES[:, :], SQ[k], start=(k == 0), stop=(k == KC - 1))
                return ins
            sc.op(sc.pe, mm, reads=bSQ + [self.bONES], writes=[self.bPS[pi]])
            i2 = t % 2
            sc.op(sc.act, lambda e, pi=pi, i2=i2: e.activation(out=TMP[i2], in_=self.PS[pi][:, :], func=AF.Ln,
                                                               scale=1.0 / D, bias=EPS),
                  reads=[self.bPS[pi]], writes=[bTMP[i2]])
            sc.op(sc.act, lambda e, i2=i2: e.activation(out=RSTD[i2], in_=TMP[i2], func=AF.Exp, scale=-0.5),
                  reads=[bTMP[i2]], writes=[bRSTD[i2]])
            for k in range(KC):
                sc.op(sc.dve, lambda e, k=k, ts=ts, i2=i2: e.scalar_tensor_tensor(
                    out=self.XN[:, k, ts], in0=self.X[:, k, ts], scalar=self.VEC[:, gcol + k:gcol + k + 1],
                    in1=RSTD[i2], op0=ALU.mult, op1=ALU.mult),
                    reads=[self.bX[k][t], bRSTD[i2], self.bVEC], writes=[self.bXN[k][t]])

    def ffn(self, layer, which):
        sc = self.sc
        ar = self.ar
        ar.reset()
        win = self.dram["l%d_%s_win" % (layer, which)]
        wout = self.dram["l%d_%s_wout" % (layer, which)]
        gcol = V_NORM["l%d_%s_norm" % (layer, which)]
        ACTB = ar.alloc([6, S], BF16)
        bACT = [[Buf("act%d_%d" % (j, t)) for t in range(NT)] for j in range(6)]
        SG = [ar.alloc([TW], F32) for _ in range(3)]
        bSG = [Buf("sg%d" % i) for i in range(3)]
        sgi = 0
        LOOK = 2
        slots = {}
        for j in range(LOOK):
            slots[j] = self.load_win(win[j])
        self.rmsnorm(gcol)
        for (j0, nj) in ((0, 6), (6, 6), (12, 5), (17, 5)):
            wo_slot = None
            for jj in range(nj):
                j = j0 + jj
                if jj == 1:
                    wo_slot = self.load_wout(wout[:, j0:j0 + nj, :], nj)
                if j + LOOK < NFC:
                    slots[j + LOOK] = self.load_win(win[j + LOOK])
                ws = slots[j]
                for t in range(NT):
                    ts = self.tsl(t)
                    pg = self.next_ps()
                    pu = self.next_ps()

                    def mm(e, ws=ws, ts=ts, pg=pg, pu=pu):
                        for k in range(KC):
                            e.matmul(self.PS[pg][:, :], self.WIN[ws][:, k, 0:128], self.XN[:, k, ts],
                                     start=(k == 0), stop=(k == KC - 1))
                        for k in range(KC):
                            ins = e.matmul(self.PS[pu][:, :], self.WIN[ws][:, k, 128:256], self.XN[:, k, ts],
                                           start=(k == 0), stop=(k == KC - 1))
                        return ins
                    sc.op(sc.pe, mm, reads=[self.bWIN[ws]] + [self.bXN[k][t] for k in range(KC)],
                          writes=[self.bPS[pg], self.bPS[pu]])
                    si = sgi
                    sgi = (sgi + 1) % 3
                    sc.op(sc.act, lambda e, pg=pg, si=si: e.activation(out=SG[si], in_=self.PS[pg][:, :], func=AF.Silu),
                          reads=[self.bPS[pg]], writes=[bSG[si]])
                    sc.op(sc.dve, lambda e, pu=pu, si=si, jj=jj, ts=ts: e.tensor_tensor(
                        out=ACTB[:, jj, ts], in0=SG[si], in1=self.PS[pu][:, :], op=ALU.mult),
                        reads=[bSG[si], self.bPS[pu]], writes=[bACT[jj][t]])
            for dm in range(KC):
                for t in range(NT):
                    ts = self.tsl(t)
                    po = self.next_ps()

                    def mm2(e, dm=dm, ts=ts, po=po, wo_slot=wo_slot, nj=nj):
                        for jj in range(nj):
                            ins = e.matmul(self.PS[po][:, :], self.WOUT[wo_slot][:, jj, dm * 128:(dm + 1) * 128],
                                           ACTB[:, jj, ts], start=(jj == 0), stop=(jj == nj - 1))
                        return ins
                    sc.op(sc.pe, mm2, reads=[self.bWOUT[wo_slot]] + [bACT[jj][t] for jj in range(nj)],
                          writes=[self.bPS[po]])
                    sc.op(sc.dve, lambda e, dm=dm, ts=ts, po=po: e.scalar_tensor_tensor(
                        out=self.X[:, dm, ts], in0=self.PS[po][:, :], scalar=0.5, in1=self.X[:, dm, ts],
                        op0=ALU.mult, op1=ALU.add),
                        reads=[self.bPS[po], self.bX[dm][t]], writes=[self.bX[dm][t]])

    def load_slot(self, kind, make_pairs):
        if kind == "win":
            i = self.win_i
            self.win_i = (self.win_i + 1) % 3
            self.sc.dma(self.sc.pool, make_pairs(self.WINF[i]), writes=[self.bWIN[i]])
        else:
            i = self.wout_i
            self.wout_i = (self.wout_i + 1) % 2
            self.sc.dma(self.sc.pool, make_pairs(self.WOUTF[i]), writes=[self.bWOUT[i]])
        return i

    def attn(self):
        sc = self.sc
        ar = self.ar
        ar.reset()
        wqkv = self.dram["wqkv"]
        wo = self.dram["wo_sb"]
        NEGTRI = self.CST[:, 0, :]
        MASK = self.CST[:, 1, :]
        BONES = self.CST[:, 2, :]

        def ld_qkv(hp):
            return self.load_slot("wout", lambda W: [
                (W[:, 0:3072].rearrange("p (k c) -> p k c", c=384), wqkv[hp][:, :, :])])

        def ld_wo(hp):
            return self.load_slot("win", lambda W: [(W[:, 0:D], wo[:, hp, :])])

        qkv_slot = {0: ld_qkv(0)}
        wo_slot = {0: ld_wo(0)}

        mark = ar.off
        self.rmsnorm(V_NORM["l0_mix_norm"])
        sc.barrier()
        ar.off = mark

        QA = [ar.alloc([S], BF16) for _ in range(2)]
        QB = [ar.alloc([S], BF16) for _ in range(2)]
        KT = [ar.alloc([S], BF16) for _ in range(2)]
        VV = [ar.alloc([16, 128], BF16) for _ in range(2)]
        OP = [ar.alloc([S], BF16) for _ in range(2)]
        bQ = [[Buf("q%d_%d" % (i, t)) for t in range(NT)] for i in range(2)]
        bK = [[Buf("k%d_%d" % (i, t)) for t in range(NT)] for i in range(2)]
        bV = [[Buf("v%d_%d" % (i, g)) for g in range(4)] for i in range(2)]
        bO = [[Buf("o%d_%d" % (i, t)) for t in range(NT)] for i in range(2)]
        SQ2 = [ar.alloc([TW], BF16) for _ in range(2)]
        bSQ2 = [Buf("sq2_%d" % i) for i in range(2)]
        TMP2 = [ar.alloc([TW], F32) for _ in range(2)]
        bTMP2 = [Buf("tmp2_%d" % i) for i in range(2)]
        RS2 = [ar.alloc([TW], F32) for _ in range(2)]
        bRS2 = [Buf("rs2_%d" % i) for i in range(2)]
        E = [ar.alloc([TW], F32) for _ in range(2)]
        bE = [Buf("e%d" % i) for i in range(2)]
        SP = [ar.alloc([TW], BF16) for _ in range(3)]
        bSP = [Buf("sp%d" % i) for i in range(3)]
        SS = [ar.alloc([TW], BF16) for _ in range(2)]
        bSS = [Buf("ss%d" % i) for i in range(2)]
        WW = [ar.alloc([TW], BF16) for _ in range(3)]
        bWW = [Buf("ww%d" % i) for i in range(3)]

        GQ = self.SMALL[:, 0:1]
        sc.op(sc.dve, lambda e: e.tensor_scalar(out=GQ, in0=self.VEC[:, V_QG:V_QG + 1], scalar1=0.125, scalar2=None,
                                                op0=ALU.mult), reads=[self.bVEC], writes=[self.bSMALL])
        GK = self.VEC[:, V_KG:V_KG + 1]
        for i in range(2):
            sc.op(sc.dve, lambda e, i=i: e.memset(QA[i][64:128, :], 0.0), writes=bQ[i])
            sc.op(sc.dve, lambda e, i=i: e.memset(QB[i][0:64, :], 0.0), writes=bQ[i])

        PZ = [0, 1]
        PC = [2, 3]
        POB = [4, 5]
        PJ = [6, 7]
        cnt = {"pj": 0, "n2": 0}

        def pj():
            cnt["pj"] += 1
            return PJ[cnt["pj"] % 2]

        def project(hp):
            st = hp % 2
            ws = qkv_slot[hp]
            WQ = self.WOUTF[ws][:, 0:3072].rearrange("p (k c) -> p k c", c=384)
            for t in range(NT):
                ts = self.tsl(t)
                for which in (0, 1):
                    pr = pj()

                    def mm(e, pr=pr, which=which, ts=ts):
                        for k in range(KC):
                            ins = e.matmul(self.PS[pr][:, :], WQ[:, k, which * 128:(which + 1) * 128], self.XN[:, k, ts],
                                           start=(k == 0), stop=(k == KC - 1))
                        return ins
                    sc.op(sc.pe, mm, reads=[self.bWOUT[ws]] + [self.bXN[k][t] for k in range(KC)], writes=[self.bPS[pr]])
                    n2 = cnt["n2"] % 2
                    cnt["n2"] += 1
                    sc.op(sc.act, lambda e, pr=pr, n2=n2: e.activation(out=SQ2[n2], in_=self.PS[pr][:, :], func=AF.Square),
                          reads=[self.bPS[pr]], writes=[bSQ2[n2]])
                    p2 = pj()
                    sc.op(sc.pe, lambda e, p2=p2, n2=n2: e.matmul(self.PS[p2][:, :], BONES, SQ2[n2], start=True, stop=True),
                          reads=[bSQ2[n2], self.bCST], writes=[self.bPS[p2]])
                    sc.op(sc.act, lambda e, p2=p2, n2=n2: e.activation(out=TMP2[n2], in_=self.PS[p2][:, :], func=AF.Ln,
                                                                       scale=1.0 / 64, bias=EPS),
                          reads=[self.bPS[p2]], writes=[bTMP2[n2]])
                    sc.op(sc.act, lambda e, n2=n2: e.activation(out=RS2[n2], in_=TMP2[n2], func=AF.Exp, scale=-0.5),
                          reads=[bTMP2[n2]], writes=[bRS2[n2]])
                    if which == 0:
                        sc.op(sc.dve, lambda e, pr=pr, n2=n2, ts=ts: e.scalar_tensor_tensor(
                            out=QA[st][0:64, ts], in0=self.PS[pr][0:64, :], scalar=GQ[0:64, :], in1=RS2[n2][0:64, :],
                            op0=ALU.mult, op1=ALU.mult), reads=[self.bPS[pr], bRS2[n2], self.bSMALL], writes=[bQ[st][t]])
                        sc.op(sc.dve, lambda e, pr=pr, n2=n2, ts=ts: e.scalar_tensor_tensor(
                            out=QB[st][64:128, ts], in0=self.PS[pr][64:128, :], scalar=GQ[64:128, :], in1=RS2[n2][64:128, :],
                            op0=ALU.mult, op1=ALU.mult), reads=[self.bPS[pr], bRS2[n2], self.bSMALL], writes=[bQ[st][t]])
                    else:
                        sc.op(sc.dve, lambda e, pr=pr, n2=n2, ts=ts: e.scalar_tensor_tensor(
                            out=KT[st][:, ts], in0=self.PS[pr][:, :], scalar=GK, in1=RS2[n2],
                            op0=ALU.mult, op1=ALU.mult), reads=[self.bPS[pr], bRS2[n2], self.bVEC], writes=[bK[st][t]])
            for g in range(4):
                pr = pj()

                def mmv(e, pr=pr, g=g):
                    for q4 in range(4):
                        scn = g * 4 + q4
                        for k in range(KC):
                            ins = e.matmul(self.PS[pr][:, q4 * 128:(q4 + 1) * 128], self.XN[:, k, scn * 128:(scn + 1) * 128],
                                           WQ[:, k, 256:384], start=(k == 0), stop=(k == KC - 1))
                    return ins
                sc.op(sc.pe, mmv, reads=[self.bWOUT[ws]] + [self.bXN[k][g] for k in range(KC)], writes=[self.bPS[pr]])
                sc.op(sc.act, lambda e, pr=pr, g=g: e.activation(
                    out=VV[st][:, g * 4:(g + 1) * 4, :].rearrange("p a b -> p (a b)"), in_=self.PS[pr][:, :], func=AF.Copy),
                    reads=[self.bPS[pr]], writes=[bV[st][g]])

        def core(hp):
            st = hp % 2
            items = []
            for hl in range(2):
                for t in range(NT):
                    for c in range(4 * t + 3, -1, -1):
                        items.append((hl, t, c))
            n = len(items)
            state = {}

            def stageA(i):
                hl, t, c = items[i]
                lo = max(0, 128 * (c - 4 * t))
                diag = c >= 4 * t
                first = (c == 4 * t + 3)
                gi = (hl * NT + t) % 2
                Qp = QA[st] if hl == 0 else QB[st]
                qs = slice(t * TW + lo, (t + 1) * TW)
                pz = PZ[i % 2]
                ei = i % 2
                si = i % 3
                if first:
                    sc.op(sc.dve, lambda e, gi=gi: e.memset(SS[gi], 0.0), writes=[bSS[gi]])
                sc.op(sc.pe, lambda e, pz=pz, c=c, lo=lo, qs=qs, Qp=Qp: e.matmul(
                    self.PS[pz][:, lo:TW], KT[st][:, c * 128:(c + 1) * 128], Qp[:, qs], start=True, stop=True),
                    reads=[bK[st][c // 4], bQ[st][t]], writes=[self.bPS[pz]])
                sc.op(sc.act, lambda e, pz=pz, ei=ei, lo=lo: e.activation(out=E[ei][:, lo:TW], in_=self.PS[pz][:, lo:TW],
                                                                          func=AF.Exp),
                      reads=[self.bPS[pz]], writes=[bE[ei]])
                sc.op(sc.act, lambda e, ei=ei, si=si, lo=lo: e.activation(out=SP[si][:, lo:TW], in_=E[ei][:, lo:TW],
                                                                          func=AF.Ln, bias=1.0),
                      reads=[bE[ei]], writes=[bSP[si]])
                if diag:
                    sc.op(sc.dve, lambda e, si=si, lo=lo: e.tensor_tensor(out=SP[si][:, lo:lo + 128], in0=SP[si][:, lo:lo + 128],
                                                                          in1=MASK, op=ALU.mult),
                          reads=[bSP[si], self.bCST], writes=[bSP[si]])

            def stageB(i):
                hl, t, c = items[i]
                lo = max(0, 128 * (c - 4 * t))
                diag = c >= 4 * t
                first = (c == 4 * t + 3)
                gi = (hl * NT + t) % 2
                Qp = QA[st] if hl == 0 else QB[st]
                qs = slice(t * TW + lo, (t + 1) * TW)
                pc = PC[i % 2]
                si = i % 3
                wi = i % 3

                def mm(e):
                    e.matmul(self.PS[pc][:, lo:TW], NEGTRI, SP[si][:, lo:TW], start=True, stop=False)
                    if not first:
                        e.matmul(self.PS[pc][:, lo:TW], self.NEGONES[:, :], SS[gi][:, lo:TW], start=False, stop=False)
                    return e.matmul(self.PS[pc][:, lo:TW], KT[st][:, c * 128:(c + 1) * 128], Qp[:, qs], start=False, stop=True)
                sc.op(sc.pe, mm, reads=[bSP[si], bSS[gi], self.bCST, self.bONES, bK[st][c // 4], bQ[st][t]],
                      writes=[self.bPS[pc]])
                sc.op(sc.act, lambda e: e.activation(out=WW[wi][:, lo:TW], in_=self.PS[pc][:, lo:TW], func=AF.Exp),
                      reads=[self.bPS[pc]], writes=[bWW[wi]])
                if diag:
                    sc.op(sc.dve, lambda e: e.tensor_tensor(out=WW[wi][:, lo:lo + 128], in0=WW[wi][:, lo:lo + 128],
                                                            in1=MASK, op=ALU.mult),
                          reads=[bWW[wi], self.bCST], writes=[bWW[wi]])
                if c > 0:
                    sc.op(sc.dve, lambda e: e.tensor_tensor(out=SS[gi][:, lo:TW], in0=SS[gi][:, lo:TW], in1=SP[si][:, lo:TW],
                                                            op=ALU.add),
                          reads=[bSS[gi], bSP[si]], writes=[bSS[gi]])

            def stageC(i):
                hl, t, c = items[i]
                lo = max(0, 128 * (c - 4 * t))
                first = (c == 4 * t + 3)
                gi = (hl * NT + t) % 2
                po = POB[gi]
                wi = i % 3
                sc.op(sc.pe, lambda e: e.matmul(self.PS[po][:, lo:TW], VV[st][:, c, :], WW[wi][:, lo:TW],
                                                start=first, stop=(c == 0)),
                      reads=[bV[st][c // 4], bWW[wi]], writes=[self.bPS[po]])
                if c == 0:
                    rows = slice(hl * 64, (hl + 1) * 64)
                    sc.op(sc.dve, lambda e: e.tensor_copy(out=OP[st][rows, self.tsl(t)], in_=self.PS[po][rows, :]),
                          reads=[self.bPS[po]], writes=[bO[st][t]])

            for step in range(n + 2):
                if step < n:
                    stageA(step)
                if 0 <= step - 1 < n:
                    stageB(step - 1)
                if 0 <= step - 2 < n:
                    stageC(step - 2)

        def oproj(hp):
            st = hp % 2
            ws = wo_slot[hp]
            WO = self.WINF[ws]
            for dm in range(KC):
                for t in range(NT):
                    pr = pj()
                    sc.op(sc.pe, lambda e, pr=pr, dm=dm, t=t: e.matmul(self.PS[pr][:, :], WO[:, dm * 128:(dm + 1) * 128],
                                                                      OP[st][:, self.tsl(t)], start=True, stop=True),
                          reads=[self.bWIN[ws], bO[st][t]], writes=[self.bPS[pr]])
                    sc.op(sc.dve, lambda e, pr=pr, dm=dm, t=t: e.tensor_tensor(
                        out=self.X[:, dm, self.tsl(t)], in0=self.PS[pr][:, :], in1=self.X[:, dm, self.tsl(t)], op=ALU.add),
                        reads=[self.bPS[pr], self.bX[dm][t]], writes=[self.bX[dm][t]])

        for hp in range(8):
            if hp + 1 < 8:
                qkv_slot[hp + 1] = ld_qkv(hp + 1)
                wo_slot[hp + 1] = ld_wo(hp + 1)
            project(hp)
            core(hp)
            oproj(hp)

    def lru(self):
        sc = self.sc
        ar = self.ar
        ar.reset()
        win = self.dram["lru_win"]
        wo = self.dram["lru_wo"]
        slots = {0: self.load_win(win[0]), 1: self.load_win(win[1])}
        mark = ar.off
        self.rmsnorm(V_NORM["l1_mix_norm"])
        sc.barrier()
        ar.off = mark
        wos = {}
        wos[0] = self.load_slot("wout", lambda W: [(W[:, 0:4 * D].rearrange("p (j d) -> p j d", d=D), wo[:, 0:4, :])])
        wos[1] = self.load_slot("wout", lambda W: [(W[:, 0:4 * D].rearrange("p (j d) -> p j d", d=D), wo[:, 4:8, :])])
        WR = self.WOUTF[wos[0]][:, 4096:5120].rearrange("p (a b) -> p a b", b=128)
        WI = self.WOUTF[wos[0]][:, 5120:6144].rearrange("p (a b) -> p a b", b=128)
        bWG = Buf("wgate")
        sc.op(sc.dve, lambda e: e.memset(WR, 0.0), writes=[bWG])
        sc.op(sc.dve, lambda e: e.memset(WI, 0.0), writes=[bWG])
        pairs = []
        for (Wt, nm) in ((WR, "lru_wr"), (WI, "lru_wi")):
            src = self.dram[nm]
            pairs.append((Wt[0:64, :, 0:64], src[0:64, :, :]))
            pairs.append((Wt[64:128, :, 64:128], src[64:128, :, :]))
        sc.dma(sc.pool, pairs, writes=[bWG])
        C1 = self.SMALL[:, 8:16]
        C2 = self.SMALL[:, 16:24]
        TS = self.SMALL[:, 24:32]
        sc.op(sc.act, lambda e: e.activation(out=TS, in_=self.VEC[:, V_LAM:V_LAM + 8], func=AF.Exp, scale=-1.0),
              reads=[self.bVEC], writes=[self.bSMALL])
        sc.op(sc.act, lambda e: e.activation(out=TS, in_=TS, func=AF.Ln, bias=1.0), reads=[self.bSMALL], writes=[self.bSMALL])
        sc.op(sc.dve, lambda e: e.tensor_scalar(out=C1, in0=TS, scalar1=-8.0, scalar2=None, op0=ALU.mult),
              reads=[self.bSMALL], writes=[self.bSMALL])
        sc.op(sc.dve, lambda e: e.tensor_scalar(out=C2, in0=TS, scalar1=-16.0, scalar2=None, op0=ALU.mult),
              reads=[self.bSMALL], writes=[self.bSMALL])

        PAD = 4
        B1 = ar.alloc([S + PAD], F32)
        B2 = ar.alloc([S], F32)
        B3 = ar.alloc([S], F32)
        B4 = ar.alloc([S], F32)
        b1, b2, b3, b4 = Buf("B1"), Buf("B2"), Buf("B3"), Buf("B4")
        HY = ar.alloc([KC, S], BF16)
        bHY = [Buf("hy%d" % j) for j in range(KC)]
        XCB = [ar.alloc([TW], BF16) for _ in range(2)]
        bXCB = [Buf("xcb%d" % i) for i in range(2)]
        IT = [ar.alloc([TW], F32) for _ in range(2)]
        bIT = [Buf("it%d" % i) for i in range(2)]
        sc.op(sc.dve, lambda e: e.memset(B1[:, 0:PAD], 0.0), writes=[b1])
        XB = B1[:, PAD:PAD + S]
        for j in range(KC):
            if j + 2 < KC:
                slots[j + 2] = self.load_win(win[j + 2])
            ws = slots[j]
            for t in range(NT):
                ts = self.tsl(t)
                px = self.next_ps()
                py = self.next_ps()

                def mm(e, ws=ws, ts=ts, px=px, py=py):
                    for k in range(KC):
                        e.matmul(self.PS[px][:, :], self.WIN[ws][:, k, 0:128], self.XN[:, k, ts], start=(k == 0), stop=(k == KC - 1))
                    for k in range(KC):
                        ins = e.matmul(self.PS[py][:, :], self.WIN[ws][:, k, 128:256], self.XN[:, k, ts], start=(k == 0),
                                       stop=(k == KC - 1))
                    return ins
                sc.op(sc.pe, mm, reads=[self.bWIN[ws]] + [self.bXN[k][t] for k in range(KC)], writes=[self.bPS[px], self.bPS[py]])
                sc.op(sc.act, lambda e, px=px, ts=ts: e.activation(out=XB[:, ts], in_=self.PS[px][:, :], func=AF.Copy),
                      reads=[self.bPS[px]], writes=[b1])
                sc.op(sc.act, lambda e, py=py, ts=ts: e.activation(out=B2[:, ts], in_=self.PS[py][:, :], func=AF.Copy),
                      reads=[self.bPS[py]], writes=[b2])
            sc.op(sc.dve, lambda e: e.tensor_tensor(out=B4, in0=B2, in1=B2, op=ALU.mult), reads=[b2], writes=[b4])
            sc.op(sc.dve, lambda e: e.tensor_scalar(out=B4, in0=B4, scalar1=0.044715, scalar2=1.0, op0=ALU.mult, op1=ALU.add),
                  reads=[b4], writes=[b4])
            sc.op(sc.dve, lambda e: e.tensor_tensor(out=B4, in0=B4, in1=B2, op=ALU.mult), reads=[b4, b2], writes=[b4])
            sc.op(sc.act, lambda e: e.activation(out=B4, in_=B4, func=AF.Sigmoid, scale=1.5957691216057308),
                  reads=[b4], writes=[b4])
            sc.op(sc.dve, lambda e: e.tensor_tensor(out=B2, in0=B2, in1=B4, op=ALU.mult), reads=[b4, b2], writes=[b2])
            cw = lambda tap, j=j: self.VEC[:, V_CONVW + tap * 8 + j: V_CONVW + tap * 8 + j + 1]
            cb = self.VEC[:, V_CONVB + j:V_CONVB + j + 1]
            sc.op(sc.dve, lambda e, cw=cw, cb=cb: e.tensor_scalar(out=B3, in0=B1[:, 1:1 + S], scalar1=cw(0), scalar2=cb,
                                                                  op0=ALU.mult, op1=ALU.add),
                  reads=[b1, self.bVEC], writes=[b3])
            for tap in (1, 2, 3):
                sc.op(sc.dve, lambda e, cw=cw, tap=tap: e.scalar_tensor_tensor(
                    out=B3, in0=B1[:, 1 + tap:1 + tap + S], scalar=cw(tap), in1=B3, op0=ALU.mult, op1=ALU.add),
                    reads=[b1, b3, self.bVEC], writes=[b3])
            for t in range(NT):
                ts = self.tsl(t)
                xi = t % 2
                sc.op(sc.act, lambda e, xi=xi, ts=ts: e.activation(out=XCB[xi], in_=B3[:, ts], func=AF.Copy),
                      reads=[b3], writes=[bXCB[xi]])
                pr = self.next_ps()
                pi = self.next_ps()
                sc.op(sc.pe, lambda e, pr=pr, xi=xi, j=j: e.matmul(self.PS[pr][:, :], WR[:, j, :], XCB[xi], start=True, stop=True),
                      reads=[bWG, bXCB[xi]], writes=[self.bPS[pr]])
                sc.op(sc.pe, lambda e, pi=pi, xi=xi, j=j: e.matmul(self.PS[pi][:, :], WI[:, j, :], XCB[xi], start=True, stop=True),
                      reads=[bWG, bXCB[xi]], writes=[self.bPS[pi]])
                sc.op(sc.act, lambda e, pr=pr, ts=ts, j=j: e.activation(out=XB[:, ts], in_=self.PS[pr][:, :], func=AF.Sigmoid,
                                                                        bias=self.VEC[:, V_BR + j:V_BR + j + 1]),
                      reads=[self.bPS[pr], b3, self.bVEC], writes=[b1])
                sc.op(sc.act, lambda e, pi=pi, xi=xi, j=j: e.activation(out=IT[xi], in_=self.PS[pi][:, :], func=AF.Sigmoid,
                                                                        bias=self.VEC[:, V_BI + j:V_BI + j + 1]),
                      reads=[self.bPS[pi], self.bVEC], writes=[bIT[xi]])
                sc.op(sc.dve, lambda e, xi=xi, ts=ts: e.tensor_tensor(out=B3[:, ts], in0=B3[:, ts], in1=IT[xi], op=ALU.mult),
                      reads=[b3, bIT[xi], bXCB[xi]], writes=[b3])
            sc.op(sc.act, lambda e, j=j: e.activation(out=B4, in_=XB, func=AF.Exp, scale=C2[:, j:j + 1]),
                  reads=[b1, self.bSMALL, b2], writes=[b4])
            sc.op(sc.act, lambda e, j=j: e.activation(out=XB, in_=XB, func=AF.Exp, scale=C1[:, j:j + 1]),
                  reads=[b1, self.bSMALL], writes=[b1])
            sc.op(sc.act, lambda e: e.activation(out=B4, in_=B4, func=AF.Sqrt, scale=-1.0, bias=1.0), reads=[b4], writes=[b4])
            sc.op(sc.dve, lambda e: e.tensor_tensor(out=B3, in0=B3, in1=B4, op=ALU.mult), reads=[b3, b4], writes=[b3])
            sc.op(sc.dve, lambda e: e.tensor_tensor_scan(out=B4, data0=XB, data1=B3, initial=0.0, op0=ALU.mult, op1=ALU.add),
                  reads=[b1, b3], writes=[b4])
            sc.op(sc.dve, lambda e, j=j: e.tensor_tensor(out=HY[:, j, :], in0=B4, in1=B2, op=ALU.mult),
                  reads=[b4, b2], writes=[bHY[j]])
        for dm in range(KC):
            for t in range(NT):
                ts = self.tsl(t)
                po = self.next_ps()

                def mm2(e, dm=dm, ts=ts, po=po):
                    for j in range(KC):
                        W = self.WOUTF[wos[j // 4]][:, 0:4 * D].rearrange("p (j d) -> p j d", d=D)
                        ins = e.matmul(self.PS[po][:, :], W[:, j % 4, dm * 128:(dm + 1) * 128], HY[:, j, ts],
                                       start=(j == 0), stop=(j == KC - 1))
                    return ins
                sc.op(sc.pe, mm2, reads=[self.bWOUT[wos[0]], self.bWOUT[wos[1]]] + bHY, writes=[self.bPS[po]])
                sc.op(sc.dve, lambda e, dm=dm, ts=ts, po=po: e.tensor_tensor(
                    out=self.X[:, dm, ts], in0=self.PS[po][:, :], in1=self.X[:, dm, ts], op=ALU.add),
                    reads=[self.bPS[po], self.bX[dm][t]], writes=[self.bX[dm][t]])


def _chunked(w):
    c = w.shape[1]
    return np.ascontiguousarray(w.reshape(KC, 128, c).transpose(1, 0, 2))


def _win_slabs(w, nslab, half_off):
    a = _chunked(w)
    out = np.empty((nslab, 128, KC, 256), np.float32)
    for j in range(nslab):
        out[j, :, :, 0:128] = a[:, :, j * 128:(j + 1) * 128]
        out[j, :, :, 128:256] = a[:, :, half_off + j * 128: half_off + (j + 1) * 128]
    return out


def _vec_cols(v):
    return np.ascontiguousarray(v.reshape(KC, 128).T)


def prepare_shared(inp):
    sh = {}
    vecs = np.zeros((128, NV), np.float32)
    for name, col in V_NORM.items():
        vecs[:, col:col + 8] = _vec_cols(inp[name])
    for j in range(4):
        vecs[:, V_CONVW + j * 8: V_CONVW + (j + 1) * 8] = _vec_cols(inp["l1_lru_conv_w"][j])
    vecs[:, V_CONVB:V_CONVB + 8] = _vec_cols(inp["l1_lru_conv_b"])
    vecs[:, V_BR:V_BR + 8] = _vec_cols(inp["l1_lru_b_r"])
    vecs[:, V_BI:V_BI + 8] = _vec_cols(inp["l1_lru_b_i"])
    vecs[:, V_LAM:V_LAM + 8] = _vec_cols(inp["l1_lru_lambda"])
    vecs[:, V_QG] = np.tile(inp["l0_sb_q_norm"], 2)
    vecs[:, V_KG] = np.tile(inp["l0_sb_k_norm"], 2)
    sh["vecs"] = vecs
    j = np.arange(128)[:, None]
    s = np.arange(128)[None, :]
    consts = np.zeros((128, 3, 128), np.float32)
    consts[:, 0, :] = -(j >= s).astype(np.float32)
    consts[:, 1, :] = (s > j).astype(np.float32)
    consts[:, 2, :] = ((j // 64) == (s // 64)).astype(np.float32)
    sh["consts"] = consts
    for l in (0, 1):
        for f in ("ff1", "ff2"):
            sh["l%d_%s_win" % (l, f)] = _win_slabs(inp["l%d_%s_w_in" % (l, f)], NFC, FF)
            wo = inp["l%d_%s_w_out" % (l, f)]
            sh["l%d_%s_wout" % (l, f)] = np.ascontiguousarray(wo.reshape(NFC, 128, D).transpose(1, 0, 2))
    a = _chunked(inp["l0_sb_w_qkv"])
    wqkv = np.empty((8, 128, KC, 384), np.float32)
    for hp in range(8):
        for w3 in range(3):
            wqkv[hp, :, :, w3 * 128:(w3 + 1) * 128] = a[:, :, w3 * 1024 + hp * 128: w3 * 1024 + (hp + 1) * 128]
    sh["wqkv"] = wqkv
    sh["wo_sb"] = _chunked(inp["l0_sb_w_o"])
    sh["lru_win"] = _win_slabs(inp["l1_lru_w_in"], 8, 1024)
    for nm, key in (("lru_wr", "l1_lru_w_r"), ("lru_wi", "l1_lru_w_i")):
        w = inp[key]
        o = np.empty((128, 8, 64), np.float32)
        for jj in range(8):
            o[0:64, jj, :] = w[2 * jj]
            o[64:128, jj, :] = w[2 * jj + 1]
        sh[nm] = o
    sh["lru_wo"] = _chunked(inp["l1_lru_w_o"])
    return sh


def x_to_dev(xb):
    return np.ascontiguousarray(xb.T.reshape(KC, 128, S).transpose(1, 0, 2))


def y_from_dev(y):
    return np.ascontiguousarray(y.transpose(1, 0, 2).reshape(D, S).T)


FULL_PHASES = [("ffn", 0, "ff1"), ("attn",), ("ffn", 0, "ff2"),
               ("ffn", 1, "ff1"), ("lru",), ("ffn", 1, "ff2")]

_PROG_CACHE = {}
FUSED = True


def run_phases(inp, phases, xs, ncores):
    key = tuple(phases)
    if key not in _PROG_CACHE:
        _PROG_CACHE[key] = Prog(phases)
    prog = _PROG_CACHE[key]
    sh = prepare_shared(inp)
    in_maps = []
    for c in range(ncores):
        m = dict(sh)
        m["x"] = x_to_dev(xs[c])
        in_maps.append(m)
    res = run_bass_kernel_spmd(prog.nc, in_maps, core_ids=list(range(ncores)))
    return [y_from_dev(np.asarray(r["y"])) for r in res.results]


def kernel(**inputs):
    inp = {k: np.asarray(v, dtype=np.float32) for k, v in inputs.items()}
    x = inp["x"]
    if FUSED:
        outs = run_phases(inp, FULL_PHASES, [x[b] for b in range(8)], 8)
    else:
        mid = run_phases(inp, FULL_PHASES[:3], [x[b] for b in range(8)], 8)
        outs = run_phases(inp, FULL_PHASES[3:], mid, 8)
    return np.stack(outs, axis=0).astype(np.float32)
```

```python
import numpy as np
from contextlib import ExitStack
import concourse.bass as bass
import concourse.mybir as mybir
from concourse.bass_utils import run_bass_kernel_spmd

F32 = mybir.dt.float32
BF16 = mybir.dt.bfloat16
AF = mybir.ActivationFunctionType
ALU = mybir.AluOpType

S = 2048
D = 1024
KC = 8
NT = 4
TW = 512
FF = 2816
NFC = 22
EPS = 1e-6

V_NORM = {"l0_ff1_norm": 0, "l0_mix_norm": 8, "l0_ff2_norm": 16,
          "l1_ff1_norm": 24, "l1_mix_norm": 32, "l1_ff2_norm": 40}
V_CONVW = 48
V_CONVB = 80
V_BR = 88
V_BI = 96
V_LAM = 104
V_QG = 112
V_KG = 113
NV = 114


class Buf:
    __slots__ = ("name", "w", "r", "dsem", "dcnt")

    def __init__(self, name):
        self.name = name
        self.w = None
        self.r = {}
        self.dsem = None
        self.dcnt = 0


class Eng:
    def __init__(self, name, sem):
        self.name = name
        self.sem = sem
        self.cnt = 0
        self.prog = []
        self.waited = {}


class Sched:
    def __init__(self, nc, es):
        self.nc = nc
        self.es = es
        self.nsem = 0
        self.pe = Eng("pe", self.new_sem("pe"))
        self.act = Eng("act", self.new_sem("act"))
        self.dve = Eng("dve", self.new_sem("dve"))
        self.pool = Eng("pool", self.new_sem("pool"))
        self.sp = Eng("sp", self.new_sem("sp"))
        self.engs = [self.pe, self.act, self.dve, self.pool, self.sp]

    def new_sem(self, name="s"):
        self.nsem += 1
        return self.es.enter_context(self.nc.semaphore("%s_%d" % (name, self.nsem)))

    def _deps(self, eng, reads, writes):
        toks = []
        for b in reads:
            if b.w is not None:
                toks.append(b.w)
        for b in writes:
            if b.w is not None:
                toks.append(b.w)
            toks.extend(b.r.values())
        for (sem, val) in toks:
            key = id(sem)
            if eng.waited.get(key, 0) >= val:
                continue
            eng.waited[key] = val
            eng.prog.append(("wait", sem, val))

    @staticmethod
    def _mark(tok, reads, writes):
        key = id(tok[0])
        for b in reads:
            old = b.r.get(key)
            if old is None or old[1] < tok[1]:
                b.r[key] = tok
        for b in writes:
            b.w = tok
            b.r = {}

    def op(self, eng, fn, reads=(), writes=()):
        self._deps(eng, reads, writes)
        eng.cnt += 1
        tok = (eng.sem, eng.cnt)
        eng.prog.append(("op", fn, eng.sem))
        self._mark(tok, reads, writes)
        return tok

    def dma(self, eng, pairs, reads=(), writes=(), sem_buf=None):
        self._deps(eng, reads, writes)
        b = sem_buf if sem_buf is not None else writes[0]
        if b.dsem is None:
            b.dsem = self.new_sem("d")
        for (o, i) in pairs:
            b.dcnt += 16
            eng.prog.append(("dma", o, i, b.dsem))
        tok = (b.dsem, b.dcnt)
        self._mark(tok, reads, writes)
        return tok

    def wait_tok(self, eng, tok):
        key = id(tok[0])
        if eng.waited.get(key, 0) >= tok[1]:
            return
        eng.waited[key] = tok[1]
        eng.prog.append(("wait", tok[0], tok[1]))

    def barrier(self, engs=None):
        engs = engs or [self.pe, self.act, self.dve]
        toks = [(e.sem, e.cnt) for e in engs if e.cnt > 0]
        for e in engs:
            for t in toks:
                if t[0] is e.sem:
                    continue
                self.wait_tok(e, t)

    @staticmethod
    def replay(eng, e):
        for item in eng.prog:
            if item[0] == "wait":
                e.wait_ge(item[1], item[2])
            elif item[0] == "op":
                ins = item[1](e)
                ins.then_inc(item[2], 1)
            else:
                e.dma_start(out=item[1], in_=item[2]).then_inc(item[3], 16)


class Arena:
    def __init__(self, t, nelem):
        self.t = t
        self.n = nelem
        self.off = 0

    def reset(self):
        self.off = 0

    def alloc(self, shape, dtype):
        n = 1
        for s in shape:
            n *= s
        ne = n * (2 if dtype == F32 else 1)
        ne = (ne + 31) // 32 * 32
        assert self.off + ne <= self.n, ("arena overflow", self.off, ne, self.n)
        ap = self.t[:, self.off:self.off + ne]
        self.off += ne
        if dtype == F32:
            ap = ap.bitcast(F32)
        ap = ap[:, 0:n]
        if len(shape) == 2:
            ap = ap.rearrange("p (a b) -> p a b", b=shape[1])
        elif len(shape) == 3:
            ap = ap.rearrange("p (a b c) -> p a b c", b=shape[1], c=shape[2])
        return ap


class Prog:
    def __init__(self, phases):
        self.phases = phases
        nc = self.nc = bass.Bass("TRN2", target_bir_lowering=False)
        es = self.es = ExitStack()
        self.sc = Sched(nc, es)
        d = self.dram = {}

        def din(name, shape):
            d[name] = nc.dram_tensor(name, list(shape), F32, kind="ExternalInput").ap()

        din("x", [128, KC, S])
        din("vecs", [128, NV])
        din("consts", [128, 3, 128])
        for l in (0, 1):
            for f in ("ff1", "ff2"):
                din("l%d_%s_win" % (l, f), [NFC, 128, KC, 256])
                din("l%d_%s_wout" % (l, f), [128, NFC, D])
        din("wqkv", [8, 128, KC, 384])
        din("wo_sb", [128, KC, D])
        din("lru_win", [8, 128, KC, 256])
        din("lru_wr", [128, 8, 64])
        din("lru_wi", [128, 8, 64])
        din("lru_wo", [128, KC, D])
        self.y = nc.dram_tensor("y", [128, KC, S], F32, kind="ExternalOutput").ap()

        sb = lambda name, shape, dt: es.enter_context(nc.sbuf_tensor(name, shape, dt))
        self.X = sb("X", [128, KC, S], F32)
        self.XN = sb("XN", [128, KC, S], BF16)
        self.VEC = sb("VEC", [128, NV], F32)
        self.CST = sb("CST", [128, 3, 128], BF16)
        self.ONES = sb("ONES", [128, 128], BF16)
        self.NEGONES = sb("NEGONES", [128, 128], BF16)
        self.SMALL = sb("SMALL", [128, 32], F32)
        self.WINF = [sb("WIN%d" % i, [128, KC * 256], BF16) for i in range(3)]
        self.WIN = [w[:, :].rearrange("p (k c) -> p k c", c=256) for w in self.WINF]
        self.WOUTF = [sb("WOUT%d" % i, [128, 6 * D], BF16) for i in range(2)]
        self.WOUT = [w[:, :].rearrange("p (j d) -> p j d", d=D) for w in self.WOUTF]
        NSCR = 36 * 1024
        self.SCR = sb("SCR", [128, NSCR], BF16)
        self.ar = Arena(self.SCR, NSCR)
        self.PS = [es.enter_context(nc.psum_tensor("ps%d" % i, [128, TW], F32)) for i in range(8)]

        self.bX = [[Buf("X%d_%d" % (k, t)) for t in range(NT)] for k in range(KC)]
        self.bXN = [[Buf("XN%d_%d" % (k, t)) for t in range(NT)] for k in range(KC)]
        self.bPS = [Buf("ps%d" % i) for i in range(8)]
        self.bWIN = [Buf("win%d" % i) for i in range(3)]
        self.bWOUT = [Buf("wout%d" % i) for i in range(2)]
        self.bVEC = Buf("vec")
        self.bCST = Buf("cst")
        self.bONES = Buf("ones")
        self.bSMALL = Buf("small")
        self.bOUT = Buf("out")
        self.win_i = 0
        self.wout_i = 0
        self.ps_i = 0

        self.build()

    def tsl(self, t):
        return slice(t * TW, (t + 1) * TW)

    def next_ps(self):
        i = self.ps_i
        self.ps_i = (self.ps_i + 1) % 8
        return i

    def load_win(self, src):
        i = self.win_i
        self.win_i = (self.win_i + 1) % 3
        pairs = [(self.WIN[i][:, :, :], src[:, :, :])]
        self.sc.dma(self.sc.pool, pairs, writes=[self.bWIN[i]])
        return i

    def load_wout(self, src, nch):
        i = self.wout_i
        self.wout_i = (self.wout_i + 1) % 2
        pairs = [(self.WOUT[i][:, 0:nch, :], src[:, 0:nch, :])]
        self.sc.dma(self.sc.pool, pairs, writes=[self.bWOUT[i]])
        return i

    def build(self):
        sc = self.sc
        for k in range(KC):
            sc.dma(sc.sp, [(self.X[:, k, :], self.dram["x"][:, k, :])], writes=self.bX[k])
        sc.dma(sc.sp, [(self.VEC[:, :], self.dram["vecs"][:, :])], writes=[self.bVEC])
        sc.dma(sc.pool, [(self.CST[:, :, :], self.dram["consts"][:, :, :])], writes=[self.bCST])
        sc.op(sc.dve, lambda e: e.memset(self.ONES[:, :], 1.0), writes=[self.bONES])
        sc.op(sc.dve, lambda e: e.memset(self.NEGONES[:, :], -1.0), writes=[self.bONES])

        for ph in self.phases:
            if ph[0] == "ffn":
                self.ffn(ph[1], ph[2])
            elif ph[0] == "attn":
                self.attn()
            elif ph[0] == "lru":
                self.lru()
            sc.barrier()

        for k in range(KC):
            sc.dma(sc.sp, [(self.y[:, k, :], self.X[:, k, :])], reads=self.bX[k], sem_buf=self.bOUT)
        sc.wait_tok(sc.sp, (self.bOUT.dsem, self.bOUT.dcnt))

        with self.nc.Block() as block:
            @block.tensor
            def _(e):
                Sched.replay(sc.pe, e)

            @block.scalar
            def _(e):
                Sched.replay(sc.act, e)

            @block.vector
            def _(e):
                Sched.replay(sc.dve, e)

            @block.gpsimd
            def _(e):
                Sched.replay(sc.pool, e)

            @block.sync
            def _(e):
                Sched.replay(sc.sp, e)
        self.es.close()

    def rmsnorm(self, gcol):
        sc = self.sc
        ar = self.ar
        SQ = [ar.alloc([TW], BF16) for _ in range(KC)]
        bSQ = [Buf("sq%d" % k) for k in range(KC)]
        TMP = [ar.alloc([TW], F32) for _ in range(2)]
        bTMP = [Buf("tmp%d" % i) for i in range(2)]
        RSTD = [ar.alloc([TW], F32) for _ in range(2)]
        bRSTD = [Buf("rstd%d" % i) for i in range(2)]
        for t in range(NT):
            ts = self.tsl(t)
            for k in range(KC):
                sc.op(sc.act, lambda e, k=k, ts=ts: e.activation(out=SQ[k], in_=self.X[:, k, ts], func=AF.Square),
                      reads=[self.bX[k][t]], writes=[bSQ[k]])
            pi = self.next_ps()

            def mm(e, pi=pi):
                for k in range(KC):
                    ins = e.matmul(self.PS[pi][:, :], self.ONES[:, :], SQ[k], start=(k == 0), stop=(k == KC - 1))
                return ins
            sc.op(sc.pe, mm, reads=bSQ + [self.bONES], writes=[self.bPS[pi]])
            i2 = t % 2
            sc.op(sc.act, lambda e, pi=pi, i2=i2: e.activation(out=TMP[i2], in_=self.PS[pi][:, :], func=AF.Ln,
                                                               scale=1.0 / D, bias=EPS),
                  reads=[self.bPS[pi]], writes=[bTMP[i2]])
            sc.op(sc.act, lambda e, i2=i2: e.activation(out=RSTD[i2], in_=TMP[i2], func=AF.Exp, scale=-0.5),
                  reads=[bTMP[i2]], writes=[bRSTD[i2]])
            for k in range(KC):
                sc.op(sc.dve, lambda e, k=k, ts=ts, i2=i2: e.scalar_tensor_tensor(
                    out=self.XN[:, k, ts], in0=self.X[:, k, ts], scalar=self.VEC[:, gcol + k:gcol + k + 1],
                    in1=RSTD[i2], op0=ALU.mult, op1=ALU.mult),
                    reads=[self.bX[k][t], bRSTD[i2], self.bVEC], writes=[self.bXN[k][t]])

    def ffn(self, layer, which):
        sc = self.sc
        ar = self.ar
        ar.reset()
        win = self.dram["l%d_%s_win" % (layer, which)]
        wout = self.dram["l%d_%s_wout" % (layer, which)]
        gcol = V_NORM["l%d_%s_norm" % (layer, which)]
        ACTB = ar.alloc([6, S], BF16)
        bACT = [[Buf("act%d_%d" % (j, t)) for t in range(NT)] for j in range(6)]
        SG = [ar.alloc([TW], F32) for _ in range(3)]
        bSG = [Buf("sg%d" % i) for i in range(3)]
        sgi = 0
        LOOK = 2
        slots = {}
        for j in range(LOOK):
            slots[j] = self.load_win(win[j])
        self.rmsnorm(gcol)
        for (j0, nj) in ((0, 6), (6, 6), (12, 5), (17, 5)):
            wo_slot = None
            for jj in range(nj):
                j = j0 + jj
                if jj == 1:
                    wo_slot = self.load_wout(wout[:, j0:j0 + nj, :], nj)
                if j + LOOK < NFC:
                    slots[j + LOOK] = self.load_win(win[j + LOOK])
                ws = slots[j]
                for t in range(NT):
                    ts = self.tsl(t)
                    pg = self.next_ps()
                    pu = self.next_ps()

                    def mm(e, ws=ws, ts=ts, pg=pg, pu=pu):
                        for k in range(KC):
                            e.matmul(self.PS[pg][:, :], self.WIN[ws][:, k, 0:128], self.XN[:, k, ts],
                                     start=(k == 0), stop=(k == KC - 1))
                        for k in range(KC):
                            ins = e.matmul(self.PS[pu][:, :], self.WIN[ws][:, k, 128:256], self.XN[:, k, ts],
                                           start=(k == 0), stop=(k == KC - 1))
                        return ins
                    sc.op(sc.pe, mm, reads=[self.bWIN[ws]] + [self.bXN[k][t] for k in range(KC)],
                          writes=[self.bPS[pg], self.bPS[pu]])
                    si = sgi
                    sgi = (sgi + 1) % 3
                    sc.op(sc.act, lambda e, pg=pg, si=si: e.activation(out=SG[si], in_=self.PS[pg][:, :], func=AF.Silu),
                          reads=[self.bPS[pg]], writes=[bSG[si]])
                    sc.op(sc.dve, lambda e, pu=pu, si=si, jj=jj, ts=ts: e.tensor_tensor(
                        out=ACTB[:, jj, ts], in0=SG[si], in1=self.PS[pu][:, :], op=ALU.mult),
                        reads=[bSG[si], self.bPS[pu]], writes=[bACT[jj][t]])
            for dm in range(KC):
                for t in range(NT):
                    ts = self.tsl(t)
                    po = self.next_ps()

                    def mm2(e, dm=dm, ts=ts, po=po, wo_slot=wo_slot, nj=nj):
                        for jj in range(nj):
                            ins = e.matmul(self.PS[po][:, :], self.WOUT[wo_slot][:, jj, dm * 128:(dm + 1) * 128],
                                           ACTB[:, jj, ts], start=(jj == 0), stop=(jj == nj - 1))
                        return ins
                    sc.op(sc.pe, mm2, reads=[self.bWOUT[wo_slot]] + [bACT[jj][t] for jj in range(nj)],
                          writes=[self.bPS[po]])
                    sc.op(sc.dve, lambda e, dm=dm, ts=ts, po=po: e.scalar_tensor_tensor(
                        out=self.X[:, dm, ts], in0=self.PS[po][:, :], scalar=0.5, in1=self.X[:, dm, ts],
                        op0=ALU.mult, op1=ALU.add),
                        reads=[self.bPS[po], self.bX[dm][t]], writes=[self.bX[dm][t]])

    def load_slot(self, kind, make_pairs):
        if kind == "win":
            i = self.win_i
            self.win_i = (self.win_i + 1) % 3
            self.sc.dma(self.sc.pool, make_pairs(self.WINF[i]), writes=[self.bWIN[i]])
        else:
            i = self.wout_i
            self.wout_i = (self.wout_i + 1) % 2
            self.sc.dma(self.sc.pool, make_pairs(self.WOUTF[i]), writes=[self.bWOUT[i]])
        return i

    def attn(self):
        sc = self.sc
        ar = self.ar
        ar.reset()
        wqkv = self.dram["wqkv"]
        wo = self.dram["wo_sb"]
        NEGTRI = self.CST[:, 0, :]
        MASK = self.CST[:, 1, :]
        BONES = self.CST[:, 2, :]

        def ld_qkv(hp):
            return self.load_slot("wout", lambda W: [
                (W[:, 0:3072].rearrange("p (k c) -> p k c", c=384), wqkv[hp][:, :, :])])

        def ld_wo(hp):
            return self.load_slot("win", lambda W: [(W[:, 0:D], wo[:, hp, :])])

        qkv_slot = {0: ld_qkv(0)}
        wo_slot = {0: ld_wo(0)}

        mark = ar.off
        self.rmsnorm(V_NORM["l0_mix_norm"])
        sc.barrier()
        ar.off = mark

        QA = [ar.alloc([S], BF16) for _ in range(2)]
        QB = [ar.alloc([S], BF16) for _ in range(2)]
        KT = [ar.alloc([S], BF16) for _ in range(2)]
        VV = [ar.alloc([16, 128], BF16) for _ in range(2)]
        OP = [ar.alloc([S], BF16) for _ in range(2)]
        bQ = [[Buf("q%d_%d" % (i, t)) for t in range(NT)] for i in range(2)]
        bK = [[Buf("k%d_%d" % (i, t)) for t in range(NT)] for i in range(2)]
        bV = [[Buf("v%d_%d" % (i, g)) for g in range(4)] for i in range(2)]
        bO = [[Buf("o%d_%d" % (i, t)) for t in range(NT)] for i in range(2)]
        SQ2 = [ar.alloc([TW], BF16) for _ in range(2)]
        bSQ2 = [Buf("sq2_%d" % i) for i in range(2)]
        TMP2 = [ar.alloc([TW], F32) for _ in range(2)]
        bTMP2 = [Buf("tmp2_%d" % i) for i in range(2)]
        RS2 = [ar.alloc([TW], F32) for _ in range(2)]
        bRS2 = [Buf("rs2_%d" % i) for i in range(2)]
        E = [ar.alloc([TW], F32) for _ in range(3)]
        bE = [Buf("e%d" % i) for i in range(3)]
        ND = 6
        SP = [ar.alloc([TW], BF16) for _ in range(ND)]
        bSP = [Buf("sp%d" % i) for i in range(ND)]
        SS = [ar.alloc([TW], BF16) for _ in range(2)]
        bSS = [Buf("ss%d" % i) for i in range(2)]
        WW = [ar.alloc([TW], BF16) for _ in range(ND)]
        bWW = [Buf("ww%d" % i) for i in range(ND)]

        GQ = self.SMALL[:, 0:1]
        sc.op(sc.dve, lambda e: e.tensor_scalar(out=GQ, in0=self.VEC[:, V_QG:V_QG + 1], scalar1=0.125, scalar2=None,
                                                op0=ALU.mult), reads=[self.bVEC], writes=[self.bSMALL])
        GK = self.VEC[:, V_KG:V_KG + 1]
        for i in range(2):
            sc.op(sc.dve, lambda e, i=i: e.memset(QA[i][64:128, :], 0.0), writes=bQ[i])
            sc.op(sc.dve, lambda e, i=i: e.memset(QB[i][0:64, :], 0.0), writes=bQ[i])

        PZ = [0, 1, 2, 3]
        POB = [4, 5]
        PJ = [6, 7]
        cnt = {"pj": 0, "n2": 0}

        def pj():
            cnt["pj"] += 1
            return PJ[cnt["pj"] % 2]

        def project_units(hp):
            st = hp % 2
            ws = qkv_slot[hp]
            WQ = self.WOUTF[ws][:, 0:3072].rearrange("p (k c) -> p k c", c=384)
            units = []
            for t in range(NT):
                for which in (0, 1):
                    units.append(lambda t=t, which=which: proj_qk(hp, st, ws, WQ, t, which))
            for g in range(4):
                units.append(lambda g=g: proj_v(hp, st, ws, WQ, g))
            return units

        def proj_qk(hp, st, ws, WQ, t, which):
            if True:
                ts = self.tsl(t)
                if True:
                    pr = pj()

                    def mm(e, pr=pr, which=which, ts=ts):
                        for k in range(KC):
                            ins = e.matmul(self.PS[pr][:, :], WQ[:, k, which * 128:(which + 1) * 128], self.XN[:, k, ts],
                                           start=(k == 0), stop=(k == KC - 1))
                        return ins
                    sc.op(sc.pe, mm, reads=[self.bWOUT[ws]] + [self.bXN[k][t] for k in range(KC)], writes=[self.bPS[pr]])
                    n2 = cnt["n2"] % 2
                    cnt["n2"] += 1
                    sc.op(sc.act, lambda e, pr=pr, n2=n2: e.activation(out=SQ2[n2], in_=self.PS[pr][:, :], func=AF.Square),
                          reads=[self.bPS[pr]], writes=[bSQ2[n2]])
                    p2 = pj()
                    sc.op(sc.pe, lambda e, p2=p2, n2=n2: e.matmul(self.PS[p2][:, :], BONES, SQ2[n2], start=True, stop=True),
                          reads=[bSQ2[n2], self.bCST], writes=[self.bPS[p2]])
                    sc.op(sc.act, lambda e, p2=p2, n2=n2: e.activation(out=TMP2[n2], in_=self.PS[p2][:, :], func=AF.Ln,
                                                                       scale=1.0 / 64, bias=EPS),
                          reads=[self.bPS[p2]], writes=[bTMP2[n2]])
                    sc.op(sc.act, lambda e, n2=n2: e.activation(out=RS2[n2], in_=TMP2[n2], func=AF.Exp, scale=-0.5),
                          reads=[bTMP2[n2]], writes=[bRS2[n2]])
                    if which == 0:
                        sc.op(sc.dve, lambda e, pr=pr, n2=n2, ts=ts: e.scalar_tensor_tensor(
                            out=QA[st][0:64, ts], in0=self.PS[pr][0:64, :], scalar=GQ[0:64, :], in1=RS2[n2][0:64, :],
                            op0=ALU.mult, op1=ALU.mult), reads=[self.bPS[pr], bRS2[n2], self.bSMALL], writes=[bQ[st][t]])
                        sc.op(sc.dve, lambda e, pr=pr, n2=n2, ts=ts: e.scalar_tensor_tensor(
                            out=QB[st][64:128, ts], in0=self.PS[pr][64:128, :], scalar=GQ[64:128, :], in1=RS2[n2][64:128, :],
                            op0=ALU.mult, op1=ALU.mult), reads=[self.bPS[pr], bRS2[n2], self.bSMALL], writes=[bQ[st][t]])
                    else:
                        sc.op(sc.dve, lambda e, pr=pr, n2=n2, ts=ts: e.scalar_tensor_tensor(
                            out=KT[st][:, ts], in0=self.PS[pr][:, :], scalar=GK, in1=RS2[n2],
                            op0=ALU.mult, op1=ALU.mult), reads=[self.bPS[pr], bRS2[n2], self.bVEC], writes=[bK[st][t]])
        def proj_v(hp, st, ws, WQ, g):
            if True:
                pr = pj()

                def mmv(e, pr=pr, g=g):
                    for q4 in range(4):
                        scn = g * 4 + q4
                        for k in range(KC):
                            ins = e.matmul(self.PS[pr][:, q4 * 128:(q4 + 1) * 128], self.XN[:, k, scn * 128:(scn + 1) * 128],
                                           WQ[:, k, 256:384], start=(k == 0), stop=(k == KC - 1))
                    return ins
                sc.op(sc.pe, mmv, reads=[self.bWOUT[ws]] + [self.bXN[k][g] for k in range(KC)], writes=[self.bPS[pr]])
                sc.op(sc.act, lambda e, pr=pr, g=g: e.activation(
                    out=VV[st][:, g * 4:(g + 1) * 4, :].rearrange("p a b -> p (a b)"), in_=self.PS[pr][:, :], func=AF.Copy),
                    reads=[self.bPS[pr]], writes=[bV[st][g]])

        def core(hp, pending):
            st = hp % 2
            items = []
            for t in range(NT):
                for c in range(4 * t + 3, -1, -1):
                    for hl in range(2):
                        items.append((hl, t, c))
            n = len(items)
            every = max(1, (n + 4) // (len(pending) + 1)) if pending else 0
            state = {}

            def stageA(i):
                hl, t, c = items[i]
                lo = max(0, 128 * (c - 4 * t))
                diag = c >= 4 * t
                first = (c == 4 * t + 3)
                gi = hl
                Qp = QA[st] if hl == 0 else QB[st]
                qs = slice(t * TW + lo, (t + 1) * TW)
                pz = PZ[i % 4]
                ei = i % 3
                si = i % ND
                sc.op(sc.pe, lambda e, pz=pz, c=c, lo=lo, qs=qs, Qp=Qp: e.matmul(
                    self.PS[pz][:, lo:TW], KT[st][:, c * 128:(c + 1) * 128], Qp[:, qs], start=True, stop=True),
                    reads=[bK[st][c // 4], bQ[st][t]], writes=[self.bPS[pz]])
                sc.op(sc.act, lambda e, pz=pz, ei=ei, lo=lo: e.activation(out=E[ei][:, lo:TW], in_=self.PS[pz][:, lo:TW],
                                                                          func=AF.Exp),
                      reads=[self.bPS[pz]], writes=[bE[ei]])
                sc.op(sc.act, lambda e, ei=ei, si=si, lo=lo: e.activation(out=SP[si][:, lo:TW], in_=E[ei][:, lo:TW],
                                                                          func=AF.Ln, bias=1.0),
                      reads=[bE[ei]], writes=[bSP[si]])
                if diag:
                    sc.op(sc.dve, lambda e, si=si, lo=lo: e.tensor_tensor(out=SP[si][:, lo:lo + 128], in0=SP[si][:, lo:lo + 128],
                                                                          in1=MASK, op=ALU.mult),
                          reads=[bSP[si], self.bCST], writes=[bSP[si]])

            def stageB(i):
                hl, t, c = items[i]
                lo = max(0, 128 * (c - 4 * t))
                diag = c >= 4 * t
                first = (c == 4 * t + 3)
                gi = hl
                Qp = QA[st] if hl == 0 else QB[st]
                qs = slice(t * TW + lo, (t + 1) * TW)
                pc = PZ[i % 4]
                si = i % ND
                wi = i % ND

                if first:
                    sc.op(sc.dve, lambda e, gi=gi: e.memset(SS[gi], 0.0), writes=[bSS[gi]])

                def mm(e):
                    ins = e.matmul(self.PS[pc][:, lo:TW], NEGTRI, SP[si][:, lo:TW], start=False, stop=first,
                                   skip_group_check=True)
                    if not first:
                        ins = e.matmul(self.PS[pc][:, lo:TW], self.NEGONES[:, :], SS[gi][:, lo:TW], start=False, stop=True,
                                       skip_group_check=True)
                    return ins
                sc.op(sc.pe, mm, reads=[bSP[si], bSS[gi], self.bCST, self.bONES, self.bPS[pc]],
                      writes=[self.bPS[pc]])
                sc.op(sc.act, lambda e: e.activation(out=WW[wi][:, lo:TW], in_=self.PS[pc][:, lo:TW], func=AF.Exp),
                      reads=[self.bPS[pc]], writes=[bWW[wi]])
                if diag:
                    sc.op(sc.dve, lambda e: e.tensor_tensor(out=WW[wi][:, lo:lo + 128], in0=WW[wi][:, lo:lo + 128],
                                                            in1=MASK, op=ALU.mult),
                          reads=[bWW[wi], self.bCST], writes=[bWW[wi]])
                if c > 0:
                    sc.op(sc.dve, lambda e: e.tensor_tensor(out=SS[gi][:, lo:TW], in0=SS[gi][:, lo:TW], in1=SP[si][:, lo:TW],
                                                            op=ALU.add),
                          reads=[bSS[gi], bSP[si]], writes=[bSS[gi]])

            def stageC(i):
                hl, t, c = items[i]
                lo = max(0, 128 * (c - 4 * t))
                first = (c == 4 * t + 3)
                gi = hl
                po = POB[gi]
                wi = i % ND
                sc.op(sc.pe, lambda e: e.matmul(self.PS[po][:, lo:TW], VV[st][:, c, :], WW[wi][:, lo:TW],
                                                start=first, stop=(c == 0)),
                      reads=[bV[st][c // 4], bWW[wi]], writes=[self.bPS[po]])
                if c == 0:
                    rows = slice(hl * 64, (hl + 1) * 64)
                    sc.op(sc.dve, lambda e: e.tensor_copy(out=OP[st][rows, self.tsl(t)], in_=self.PS[po][rows, :]),
                          reads=[self.bPS[po]], writes=[bO[st][t]])

            pend = list(pending)
            DB, DC = 2, 4
            for step in range(n + DC):
                if step < n:
                    stageA(step)
                if 0 <= step - DB < n:
                    stageB(step - DB)
                if 0 <= step - DC < n:
                    stageC(step - DC)
                if pend and every and step % every == every - 1:
                    pend.pop(0)()
            while pend:
                pend.pop(0)()

        def oproj_units(hp):
            return [lambda dm=dm: oproj(hp, dm) for dm in range(KC)]

        def oproj(hp, dm):
            st = hp % 2
            ws = wo_slot[hp]
            WO = self.WINF[ws]
            if True:
                for t in range(NT):
                    pr = pj()
                    sc.op(sc.pe, lambda e, pr=pr, dm=dm, t=t: e.matmul(self.PS[pr][:, :], WO[:, dm * 128:(dm + 1) * 128],
                                                                      OP[st][:, self.tsl(t)], start=True, stop=True),
                          reads=[self.bWIN[ws], bO[st][t]], writes=[self.bPS[pr]])
                    sc.op(sc.dve, lambda e, pr=pr, dm=dm, t=t: e.tensor_tensor(
                        out=self.X[:, dm, self.tsl(t)], in0=self.PS[pr][:, :], in1=self.X[:, dm, self.tsl(t)], op=ALU.add),
                        reads=[self.bPS[pr], self.bX[dm][t]], writes=[self.bX[dm][t]])

        for u in project_units(0):
            u()
        for hp in range(8):
            pending = []
            if hp + 1 < 8:
                qkv_slot[hp + 1] = ld_qkv(hp + 1)
                wo_slot[hp + 1] = ld_wo(hp + 1)
                pending += project_units(hp + 1)
            if hp >= 1:
                pending += oproj_units(hp - 1)
            core(hp, pending)
        for u in oproj_units(7):
            u()

    def lru(self):
        sc = self.sc
        ar = self.ar
        ar.reset()
        win = self.dram["lru_win"]
        wo = self.dram["lru_wo"]
        slots = {0: self.load_win(win[0]), 1: self.load_win(win[1])}
        mark = ar.off
        self.rmsnorm(V_NORM["l1_mix_norm"])
        sc.barrier()
        ar.off = mark
        wos = {}
        wos[0] = self.load_slot("wout", lambda W: [(W[:, 0:4 * D].rearrange("p (j d) -> p j d", d=D), wo[:, 0:4, :])])
        wos[1] = self.load_slot("wout", lambda W: [(W[:, 0:4 * D].rearrange("p (j d) -> p j d", d=D), wo[:, 4:8, :])])
        WR = self.WOUTF[wos[0]][:, 4096:5120].rearrange("p (a b) -> p a b", b=128)
        WI = self.WOUTF[wos[0]][:, 5120:6144].rearrange("p (a b) -> p a b", b=128)
        bWG = Buf("wgate")
        sc.op(sc.dve, lambda e: e.memset(WR, 0.0), writes=[bWG])
        sc.op(sc.dve, lambda e: e.memset(WI, 0.0), writes=[bWG])
        pairs = []
        for (Wt, nm) in ((WR, "lru_wr"), (WI, "lru_wi")):
            src = self.dram[nm]
            pairs.append((Wt[0:64, :, 0:64], src[0:64, :, :]))
            pairs.append((Wt[64:128, :, 64:128], src[64:128, :, :]))
        sc.dma(sc.pool, pairs, writes=[bWG])
        C1 = self.SMALL[:, 8:16]
        C2 = self.SMALL[:, 16:24]
        TS = self.SMALL[:, 24:32]
        sc.op(sc.act, lambda e: e.activation(out=TS, in_=self.VEC[:, V_LAM:V_LAM + 8], func=AF.Exp, scale=-1.0),
              reads=[self.bVEC], writes=[self.bSMALL])
        sc.op(sc.act, lambda e: e.activation(out=TS, in_=TS, func=AF.Ln, bias=1.0), reads=[self.bSMALL], writes=[self.bSMALL])
        sc.op(sc.dve, lambda e: e.tensor_scalar(out=C1, in0=TS, scalar1=-8.0, scalar2=None, op0=ALU.mult),
              reads=[self.bSMALL], writes=[self.bSMALL])
        sc.op(sc.dve, lambda e: e.tensor_scalar(out=C2, in0=TS, scalar1=-16.0, scalar2=None, op0=ALU.mult),
              reads=[self.bSMALL], writes=[self.bSMALL])

        PAD = 4
        B1 = ar.alloc([S + PAD], F32)
        B2 = ar.alloc([S], F32)
        B3 = ar.alloc([S], F32)
        B4 = ar.alloc([S], F32)
        b1, b2, b3, b4 = Buf("B1"), Buf("B2"), Buf("B3"), Buf("B4")
        HY = ar.alloc([KC, S], BF16)
        bHY = [Buf("hy%d" % j) for j in range(KC)]
        XCB = [ar.alloc([TW], BF16) for _ in range(2)]
        bXCB = [Buf("xcb%d" % i) for i in range(2)]
        IT = [ar.alloc([TW], F32) for _ in range(2)]
        bIT = [Buf("it%d" % i) for i in range(2)]
        sc.op(sc.dve, lambda e: e.memset(B1[:, 0:PAD], 0.0), writes=[b1])
        XB = B1[:, PAD:PAD + S]
        for j in range(KC):
            if j + 2 < KC:
                slots[j + 2] = self.load_win(win[j + 2])
            ws = slots[j]
            for t in range(NT):
                ts = self.tsl(t)
                px = self.next_ps()
                py = self.next_ps()

                def mm(e, ws=ws, ts=ts, px=px, py=py):
                    for k in range(KC):
                        e.matmul(self.PS[px][:, :], self.WIN[ws][:, k, 0:128], self.XN[:, k, ts], start=(k == 0), stop=(k == KC - 1))
                    for k in range(KC):
                        ins = e.matmul(self.PS[py][:, :], self.WIN[ws][:, k, 128:256], self.XN[:, k, ts], start=(k == 0),
                                       stop=(k == KC - 1))
                    return ins
                sc.op(sc.pe, mm, reads=[self.bWIN[ws]] + [self.bXN[k][t] for k in range(KC)], writes=[self.bPS[px], self.bPS[py]])
                sc.op(sc.act, lambda e, px=px, ts=ts: e.activation(out=XB[:, ts], in_=self.PS[px][:, :], func=AF.Copy),
                      reads=[self.bPS[px]], writes=[b1])
                sc.op(sc.act, lambda e, py=py, ts=ts: e.activation(out=B2[:, ts], in_=self.PS[py][:, :], func=AF.Copy),
                      reads=[self.bPS[py]], writes=[b2])
            sc.op(sc.dve, lambda e: e.tensor_tensor(out=B4, in0=B2, in1=B2, op=ALU.mult), reads=[b2], writes=[b4])
            sc.op(sc.dve, lambda e: e.tensor_scalar(out=B4, in0=B4, scalar1=0.044715, scalar2=1.0, op0=ALU.mult, op1=ALU.add),
                  reads=[b4], writes=[b4])
            sc.op(sc.dve, lambda e: e.tensor_tensor(out=B4, in0=B4, in1=B2, op=ALU.mult), reads=[b4, b2], writes=[b4])
            sc.op(sc.act, lambda e: e.activation(out=B4, in_=B4, func=AF.Sigmoid, scale=1.5957691216057308),
                  reads=[b4], writes=[b4])
            sc.op(sc.dve, lambda e: e.tensor_tensor(out=B2, in0=B2, in1=B4, op=ALU.mult), reads=[b4, b2], writes=[b2])
            cw = lambda tap, j=j: self.VEC[:, V_CONVW + tap * 8 + j: V_CONVW + tap * 8 + j + 1]
            cb = self.VEC[:, V_CONVB + j:V_CONVB + j + 1]
            sc.op(sc.dve, lambda e, cw=cw, cb=cb: e.tensor_scalar(out=B3, in0=B1[:, 1:1 + S], scalar1=cw(0), scalar2=cb,
                                                                  op0=ALU.mult, op1=ALU.add),
                  reads=[b1, self.bVEC], writes=[b3])
            for tap in (1, 2, 3):
                sc.op(sc.dve, lambda e, cw=cw, tap=tap: e.scalar_tensor_tensor(
                    out=B3, in0=B1[:, 1 + tap:1 + tap + S], scalar=cw(tap), in1=B3, op0=ALU.mult, op1=ALU.add),
                    reads=[b1, b3, self.bVEC], writes=[b3])
            for t in range(NT):
                ts = self.tsl(t)
                xi = t % 2
                sc.op(sc.act, lambda e, xi=xi, ts=ts: e.activation(out=XCB[xi], in_=B3[:, ts], func=AF.Copy),
                      reads=[b3], writes=[bXCB[xi]])
                pr = self.next_ps()
                pi = self.next_ps()
                sc.op(sc.pe, lambda e, pr=pr, xi=xi, j=j: e.matmul(self.PS[pr][:, :], WR[:, j, :], XCB[xi], start=True, stop=True),
                      reads=[bWG, bXCB[xi]], writes=[self.bPS[pr]])
                sc.op(sc.pe, lambda e, pi=pi, xi=xi, j=j: e.matmul(self.PS[pi][:, :], WI[:, j, :], XCB[xi], start=True, stop=True),
                      reads=[bWG, bXCB[xi]], writes=[self.bPS[pi]])
                sc.op(sc.act, lambda e, pr=pr, ts=ts, j=j: e.activation(out=XB[:, ts], in_=self.PS[pr][:, :], func=AF.Sigmoid,
                                                                        bias=self.VEC[:, V_BR + j:V_BR + j + 1]),
                      reads=[self.bPS[pr], b3, self.bVEC], writes=[b1])
                sc.op(sc.act, lambda e, pi=pi, xi=xi, j=j: e.activation(out=IT[xi], in_=self.PS[pi][:, :], func=AF.Sigmoid,
                                                                        bias=self.VEC[:, V_BI + j:V_BI + j + 1]),
                      reads=[self.bPS[pi], self.bVEC], writes=[bIT[xi]])
                sc.op(sc.dve, lambda e, xi=xi, ts=ts: e.tensor_tensor(out=B3[:, ts], in0=B3[:, ts], in1=IT[xi], op=ALU.mult),
                      reads=[b3, bIT[xi], bXCB[xi]], writes=[b3])
            sc.op(sc.act, lambda e, j=j: e.activation(out=B4, in_=XB, func=AF.Exp, scale=C2[:, j:j + 1]),
                  reads=[b1, self.bSMALL, b2], writes=[b4])
            sc.op(sc.act, lambda e, j=j: e.activation(out=XB, in_=XB, func=AF.Exp, scale=C1[:, j:j + 1]),
                  reads=[b1, self.bSMALL], writes=[b1])
            sc.op(sc.act, lambda e: e.activation(out=B4, in_=B4, func=AF.Sqrt, scale=-1.0, bias=1.0), reads=[b4], writes=[b4])
            sc.op(sc.dve, lambda e: e.tensor_tensor(out=B3, in0=B3, in1=B4, op=ALU.mult), reads=[b3, b4], writes=[b3])
            sc.op(sc.dve, lambda e: e.tensor_tensor_scan(out=B4, data0=XB, data1=B3, initial=0.0, op0=ALU.mult, op1=ALU.add),
                  reads=[b1, b3], writes=[b4])
            sc.op(sc.dve, lambda e, j=j: e.tensor_tensor(out=HY[:, j, :], in0=B4, in1=B2, op=ALU.mult),
                  reads=[b4, b2], writes=[bHY[j]])
        for dm in range(KC):
            for t in range(NT):
                ts = self.tsl(t)
                po = self.next_ps()

                def mm2(e, dm=dm, ts=ts, po=po):
                    for j in range(KC):
                        W = self.WOUTF[wos[j // 4]][:, 0:4 * D].rearrange("p (j d) -> p j d", d=D)
                        ins = e.matmul(self.PS[po][:, :], W[:, j % 4, dm * 128:(dm + 1) * 128], HY[:, j, ts],
                                       start=(j == 0), stop=(j == KC - 1))
                    return ins
                sc.op(sc.pe, mm2, reads=[self.bWOUT[wos[0]], self.bWOUT[wos[1]]] + bHY, writes=[self.bPS[po]])
                sc.op(sc.dve, lambda e, dm=dm, ts=ts, po=po: e.tensor_tensor(
                    out=self.X[:, dm, ts], in0=self.PS[po][:, :], in1=self.X[:, dm, ts], op=ALU.add),
                    reads=[self.bPS[po], self.bX[dm][t]], writes=[self.bX[dm][t]])


def _chunked(w):
    c = w.shape[1]
    return np.ascontiguousarray(w.reshape(KC, 128, c).transpose(1, 0, 2))


def _win_slabs(w, nslab, half_off):
    a = _chunked(w)
    out = np.empty((nslab, 128, KC, 256), np.float32)
    for j in range(nslab):
        out[j, :, :, 0:128] = a[:, :, j * 128:(j + 1) * 128]
        out[j, :, :, 128:256] = a[:, :, half_off + j * 128: half_off + (j + 1) * 128]
    return out


def _vec_cols(v):
    return np.ascontiguousarray(v.reshape(KC, 128).T)


def prepare_shared(inp):
    sh = {}
    vecs = np.zeros((128, NV), np.float32)
    for name, col in V_NORM.items():
        vecs[:, col:col + 8] = _vec_cols(inp[name])
    for j in range(4):
        vecs[:, V_CONVW + j * 8: V_CONVW + (j + 1) * 8] = _vec_cols(inp["l1_lru_conv_w"][j])
    vecs[:, V_CONVB:V_CONVB + 8] = _vec_cols(inp["l1_lru_conv_b"])
    vecs[:, V_BR:V_BR + 8] = _vec_cols(inp["l1_lru_b_r"])
    vecs[:, V_BI:V_BI + 8] = _vec_cols(inp["l1_lru_b_i"])
    vecs[:, V_LAM:V_LAM + 8] = _vec_cols(inp["l1_lru_lambda"])
    vecs[:, V_QG] = np.tile(inp["l0_sb_q_norm"], 2)
    vecs[:, V_KG] = np.tile(inp["l0_sb_k_norm"], 2)
    sh["vecs"] = vecs
    j = np.arange(128)[:, None]
    s = np.arange(128)[None, :]
    consts = np.zeros((128, 3, 128), np.float32)
    consts[:, 0, :] = -(j >= s).astype(np.float32)
    consts[:, 1, :] = (s > j).astype(np.float32)
    consts[:, 2, :] = ((j // 64) == (s // 64)).astype(np.float32)
    sh["consts"] = consts
    for l in (0, 1):
        for f in ("ff1", "ff2"):
            sh["l%d_%s_win" % (l, f)] = _win_slabs(inp["l%d_%s_w_in" % (l, f)], NFC, FF)
            wo = inp["l%d_%s_w_out" % (l, f)]
            sh["l%d_%s_wout" % (l, f)] = np.ascontiguousarray(wo.reshape(NFC, 128, D).transpose(1, 0, 2))
    a = _chunked(inp["l0_sb_w_qkv"])
    wqkv = np.empty((8, 128, KC, 384), np.float32)
    for hp in range(8):
        for w3 in range(3):
            wqkv[hp, :, :, w3 * 128:(w3 + 1) * 128] = a[:, :, w3 * 1024 + hp * 128: w3 * 1024 + (hp + 1) * 128]
    sh["wqkv"] = wqkv
    sh["wo_sb"] = _chunked(inp["l0_sb_w_o"])
    sh["lru_win"] = _win_slabs(inp["l1_lru_w_in"], 8, 1024)
    for nm, key in (("lru_wr", "l1_lru_w_r"), ("lru_wi", "l1_lru_w_i")):
        w = inp[key]
        o = np.empty((128, 8, 64), np.float32)
        for jj in range(8):
            o[0:64, jj, :] = w[2 * jj]
            o[64:128, jj, :] = w[2 * jj + 1]
        sh[nm] = o
    sh["lru_wo"] = _chunked(inp["l1_lru_w_o"])
    return sh


def x_to_dev(xb):
    return np.ascontiguousarray(xb.T.reshape(KC, 128, S).transpose(1, 0, 2))


def y_from_dev(y):
    return np.ascontiguousarray(y.transpose(1, 0, 2).reshape(D, S).T)


FULL_PHASES = [("ffn", 0, "ff1"), ("attn",), ("ffn", 0, "ff2"),
               ("ffn", 1, "ff1"), ("lru",), ("ffn", 1, "ff2")]

_PROG_CACHE = {}
FUSED = True


def run_phases(inp, phases, xs, ncores):
    key = tuple(phases)
    if key not in _PROG_CACHE:
        _PROG_CACHE[key] = Prog(phases)
    prog = _PROG_CACHE[key]
    sh = prepare_shared(inp)
    in_maps = []
    for c in range(ncores):
        m = dict(sh)
        m["x"] = x_to_dev(xs[c])
        in_maps.append(m)
    res = run_bass_kernel_spmd(prog.nc, in_maps, core_ids=list(range(ncores)))
    return [y_from_dev(np.asarray(r["y"])) for r in res.results]


def kernel(**inputs):
    inp = {k: np.asarray(v, dtype=np.float32) for k, v in inputs.items()}
    x = inp["x"]
    if FUSED:
        outs = run_phases(inp, FULL_PHASES, [x[b] for b in range(8)], 8)
    else:
        mid = run_phases(inp, FULL_PHASES[:3], [x[b] for b in range(8)], 8)
        outs = run_phases(inp, FULL_PHASES[3:], mid, 8)
    return np.stack(outs, axis=0).astype(np.float32)
```

```python
import numpy as np
from contextlib import ExitStack
import concourse.bass as bass
import concourse.mybir as mybir
from concourse.bass_utils import run_bass_kernel_spmd

F32 = mybir.dt.float32
BF16 = mybir.dt.bfloat16
AF = mybir.ActivationFunctionType
ALU = mybir.AluOpType

S = 2048
D = 1024
KC = 8
NT = 4
TW = 512
FF = 2816
NFC = 22
EPS = 1e-6

V_NORM = {"l0_ff1_norm": 0, "l0_mix_norm": 8, "l0_ff2_norm": 16,
          "l1_ff1_norm": 24, "l1_mix_norm": 32, "l1_ff2_norm": 40}
V_CONVW = 48
V_CONVB = 80
V_BR = 88
V_BI = 96
V_LAM = 104
V_QG = 112
V_KG = 113
NV = 114


class Buf:
    __slots__ = ("name", "w", "r", "dsem", "dcnt")

    def __init__(self, name):
        self.name = name
        self.w = None
        self.r = {}
        self.dsem = None
        self.dcnt = 0


class Eng:
    def __init__(self, name, sem):
        self.name = name
        self.sem = sem
        self.cnt = 0
        self.prog = []
        self.waited = {}


class Sched:
    def __init__(self, nc, es):
        self.nc = nc
        self.es = es
        self.nsem = 0
        self.pe = Eng("pe", self.new_sem("pe"))
        self.act = Eng("act", self.new_sem("act"))
        self.dve = Eng("dve", self.new_sem("dve"))
        self.pool = Eng("pool", self.new_sem("pool"))
        self.sp = Eng("sp", self.new_sem("sp"))
        self.engs = [self.pe, self.act, self.dve, self.pool, self.sp]

    def new_sem(self, name="s"):
        self.nsem += 1
        return self.es.enter_context(self.nc.semaphore("%s_%d" % (name, self.nsem)))

    def _deps(self, eng, reads, writes):
        toks = []
        for b in reads:
            if b.w is not None:
                toks.append(b.w)
        for b in writes:
            if b.w is not None:
                toks.append(b.w)
            toks.extend(b.r.values())
        for (sem, val) in toks:
            key = id(sem)
            if eng.waited.get(key, 0) >= val:
                continue
            eng.waited[key] = val
            eng.prog.append(("wait", sem, val))

    @staticmethod
    def _mark(tok, reads, writes):
        key = id(tok[0])
        for b in reads:
            old = b.r.get(key)
            if old is None or old[1] < tok[1]:
                b.r[key] = tok
        for b in writes:
            b.w = tok
            b.r = {}

    def op(self, eng, fn, reads=(), writes=()):
        self._deps(eng, reads, writes)
        eng.cnt += 1
        tok = (eng.sem, eng.cnt)
        eng.prog.append(("op", fn, eng.sem))
        self._mark(tok, reads, writes)
        return tok

    def dma(self, eng, pairs, reads=(), writes=(), sem_buf=None):
        self._deps(eng, reads, writes)
        b = sem_buf if sem_buf is not None else writes[0]
        if b.dsem is None:
            b.dsem = self.new_sem("d")
        for (o, i) in pairs:
            b.dcnt += 16
            eng.prog.append(("dma", o, i, b.dsem))
        tok = (b.dsem, b.dcnt)
        self._mark(tok, reads, writes)
        return tok

    def wait_tok(self, eng, tok):
        key = id(tok[0])
        if eng.waited.get(key, 0) >= tok[1]:
            return
        eng.waited[key] = tok[1]
        eng.prog.append(("wait", tok[0], tok[1]))

    def barrier(self, engs=None):
        engs = engs or [self.pe, self.act, self.dve]
        toks = [(e.sem, e.cnt) for e in engs if e.cnt > 0]
        for e in engs:
            for t in toks:
                if t[0] is e.sem:
                    continue
                self.wait_tok(e, t)

    @staticmethod
    def replay(eng, e):
        for item in eng.prog:
            if item[0] == "wait":
                e.wait_ge(item[1], item[2])
            elif item[0] == "op":
                ins = item[1](e)
                ins.then_inc(item[2], 1)
            else:
                e.dma_start(out=item[1], in_=item[2]).then_inc(item[3], 16)


class Arena:
    def __init__(self, t, nelem):
        self.t = t
        self.n = nelem
        self.off = 0

    def reset(self):
        self.off = 0

    def alloc(self, shape, dtype):
        n = 1
        for s in shape:
            n *= s
        ne = n * (2 if dtype == F32 else 1)
        ne = (ne + 31) // 32 * 32
        assert self.off + ne <= self.n, ("arena overflow", self.off, ne, self.n)
        ap = self.t[:, self.off:self.off + ne]
        self.off += ne
        if dtype == F32:
            ap = ap.bitcast(F32)
        ap = ap[:, 0:n]
        if len(shape) == 2:
            ap = ap.rearrange("p (a b) -> p a b", b=shape[1])
        elif len(shape) == 3:
            ap = ap.rearrange("p (a b c) -> p a b c", b=shape[1], c=shape[2])
        return ap


class Prog:
    def __init__(self, phases):
        self.phases = phases
        nc = self.nc = bass.Bass("TRN2", target_bir_lowering=False)
        es = self.es = ExitStack()
        self.sc = Sched(nc, es)
        d = self.dram = {}

        def din(name, shape):
            d[name] = nc.dram_tensor(name, list(shape), F32, kind="ExternalInput").ap()

        din("x", [128, KC, S])
        din("vecs", [128, NV])
        din("consts", [128, 3, 128])
        for l in (0, 1):
            for f in ("ff1", "ff2"):
                din("l%d_%s_win" % (l, f), [NFC, 128, KC, 256])
                din("l%d_%s_wout" % (l, f), [128, NFC, D])
        din("wqkv", [8, 128, KC, 384])
        din("wo_sb", [128, KC, D])
        din("lru_win", [8, 128, KC, 256])
        din("lru_wr", [128, 8, 64])
        din("lru_wi", [128, 8, 64])
        din("lru_wo", [128, KC, D])
        self.y = nc.dram_tensor("y", [128, KC, S], F32, kind="ExternalOutput").ap()

        sb = lambda name, shape, dt: es.enter_context(nc.sbuf_tensor(name, shape, dt))
        self.X = sb("X", [128, KC, S], F32)
        self.XN = sb("XN", [128, KC, S], BF16)
        self.VEC = sb("VEC", [128, NV], F32)
        self.CST = sb("CST", [128, 3, 128], BF16)
        self.ONES = sb("ONES", [128, 128], BF16)
        self.NEGONES = sb("NEGONES", [128, 128], BF16)
        self.SMALL = sb("SMALL", [128, 32], F32)
        self.WINF = [sb("WIN%d" % i, [128, KC * 256], BF16) for i in range(3)]
        self.WIN = [w[:, :].rearrange("p (k c) -> p k c", c=256) for w in self.WINF]
        self.WOUTF = [sb("WOUT%d" % i, [128, 6 * D], BF16) for i in range(2)]
        self.WOUT = [w[:, :].rearrange("p (j d) -> p j d", d=D) for w in self.WOUTF]
        NSCR = 36 * 1024
        self.SCR = sb("SCR", [128, NSCR], BF16)
        self.ar = Arena(self.SCR, NSCR)
        self.PS = [es.enter_context(nc.psum_tensor("ps%d" % i, [128, TW], F32)) for i in range(8)]

        self.bX = [[Buf("X%d_%d" % (k, t)) for t in range(NT)] for k in range(KC)]
        self.bXN = [[Buf("XN%d_%d" % (k, t)) for t in range(NT)] for k in range(KC)]
        self.bPS = [Buf("ps%d" % i) for i in range(8)]
        self.bWIN = [Buf("win%d" % i) for i in range(3)]
        self.bWOUT = [Buf("wout%d" % i) for i in range(2)]
        self.bVEC = Buf("vec")
        self.bCST = Buf("cst")
        self.bONES = Buf("ones")
        self.bSMALL = Buf("small")
        self.bOUT = Buf("out")
        self.win_i = 0
        self.wout_i = 0
        self.ps_i = 0

        self.build()

    def tsl(self, t):
        return slice(t * TW, (t + 1) * TW)

    def next_ps(self):
        i = self.ps_i
        self.ps_i = (self.ps_i + 1) % 8
        return i

    def load_win(self, src):
        i = self.win_i
        self.win_i = (self.win_i + 1) % 3
        pairs = [(self.WIN[i][:, :, :], src[:, :, :])]
        self.sc.dma(self.sc.pool, pairs, writes=[self.bWIN[i]])
        return i

    def load_wout(self, src, nch):
        i = self.wout_i
        self.wout_i = (self.wout_i + 1) % 2
        pairs = [(self.WOUT[i][:, 0:nch, :], src[:, 0:nch, :])]
        self.sc.dma(self.sc.pool, pairs, writes=[self.bWOUT[i]])
        return i

    def build(self):
        sc = self.sc
        for k in range(KC):
            sc.dma(sc.sp, [(self.X[:, k, :], self.dram["x"][:, k, :])], writes=self.bX[k])
        sc.dma(sc.sp, [(self.VEC[:, :], self.dram["vecs"][:, :])], writes=[self.bVEC])
        sc.dma(sc.pool, [(self.CST[:, :, :], self.dram["consts"][:, :, :])], writes=[self.bCST])
        sc.op(sc.dve, lambda e: e.memset(self.ONES[:, :], 1.0), writes=[self.bONES])
        sc.op(sc.dve, lambda e: e.memset(self.NEGONES[:, :], -1.0), writes=[self.bONES])

        for ph in self.phases:
            if ph[0] == "ffn":
                self.ffn(ph[1], ph[2])
            elif ph[0] == "attn":
                self.attn()
            elif ph[0] == "lru":
                self.lru()
            sc.barrier()

        for k in range(KC):
            sc.dma(sc.sp, [(self.y[:, k, :], self.X[:, k, :])], reads=self.bX[k], sem_buf=self.bOUT)
        sc.wait_tok(sc.sp, (self.bOUT.dsem, self.bOUT.dcnt))

        with self.nc.Block() as block:
            @block.tensor
            def _(e):
                Sched.replay(sc.pe, e)

            @block.scalar
            def _(e):
                Sched.replay(sc.act, e)

            @block.vector
            def _(e):
                Sched.replay(sc.dve, e)

            @block.gpsimd
            def _(e):
                Sched.replay(sc.pool, e)

            @block.sync
            def _(e):
                Sched.replay(sc.sp, e)
        self.es.close()

    def rmsnorm(self, gcol):
        sc = self.sc
        ar = self.ar
        SQ = [ar.alloc([TW], BF16) for _ in range(KC)]
        bSQ = [Buf("sq%d" % k) for k in range(KC)]
        TMP = [ar.alloc([TW], F32) for _ in range(2)]
        bTMP = [Buf("tmp%d" % i) for i in range(2)]
        RSTD = [ar.alloc([TW], F32) for _ in range(2)]
        bRSTD = [Buf("rstd%d" % i) for i in range(2)]
        for t in range(NT):
            ts = self.tsl(t)
            for k in range(KC):
                sc.op(sc.act, lambda e, k=k, ts=ts: e.activation(out=SQ[k], in_=self.X[:, k, ts], func=AF.Square),
                      reads=[self.bX[k][t]], writes=[bSQ[k]])
            pi = self.next_ps()

            def mm(e, pi=pi):
                for k in range(KC):
                    ins = e.matmul(self.PS[pi][:, :], self.ONES[:, :], SQ[k], start=(k == 0), stop=(k == KC - 1))
                return ins
            sc.op(sc.pe, mm, reads=bSQ + [self.bONES], writes=[self.bPS[pi]])
            i2 = t % 2
            sc.op(sc.act, lambda e, pi=pi, i2=i2: e.activation(out=TMP[i2], in_=self.PS[pi][:, :], func=AF.Ln,
                                                               scale=1.0 / D, bias=EPS),
                  reads=[self.bPS[pi]], writes=[bTMP[i2]])
            sc.op(sc.act, lambda e, i2=i2: e.activation(out=RSTD[i2], in_=TMP[i2], func=AF.Exp, scale=-0.5),
                  reads=[bTMP[i2]], writes=[bRSTD[i2]])
            for k in range(KC):
                sc.op(sc.dve, lambda e, k=k, ts=ts, i2=i2: e.scalar_tensor_tensor(
                    out=self.XN[:, k, ts], in0=self.X[:, k, ts], scalar=self.VEC[:, gcol + k:gcol + k + 1],
                    in1=RSTD[i2], op0=ALU.mult, op1=ALU.mult),
                    reads=[self.bX[k][t], bRSTD[i2], self.bVEC], writes=[self.bXN[k][t]])

    def ffn(self, layer, which):
        sc = self.sc
        ar = self.ar
        ar.reset()
        win = self.dram["l%d_%s_win" % (layer, which)]
        wout = self.dram["l%d_%s_wout" % (layer, which)]
        gcol = V_NORM["l%d_%s_norm" % (layer, which)]
        ACTB = ar.alloc([6, S], BF16)
        bACT = [[Buf("act%d_%d" % (j, t)) for t in range(NT)] for j in range(6)]
        SG = [ar.alloc([TW], F32) for _ in range(3)]
        bSG = [Buf("sg%d" % i) for i in range(3)]
        sgi = 0
        LOOK = 2
        slots = {}
        for j in range(LOOK):
            slots[j] = self.load_win(win[j])
        self.rmsnorm(gcol)
        for (j0, nj) in ((0, 6), (6, 6), (12, 5), (17, 5)):
            wo_slot = None
            for jj in range(nj):
                j = j0 + jj
                if jj == 1:
                    wo_slot = self.load_wout(wout[:, j0:j0 + nj, :], nj)
                if j + LOOK < NFC:
                    slots[j + LOOK] = self.load_win(win[j + LOOK])
                ws = slots[j]
                for t in range(NT):
                    ts = self.tsl(t)
                    pg = self.next_ps()
                    pu = self.next_ps()

                    def mm(e, ws=ws, ts=ts, pg=pg, pu=pu):
                        for k in range(KC):
                            e.matmul(self.PS[pg][:, :], self.WIN[ws][:, k, 0:128], self.XN[:, k, ts],
                                     start=(k == 0), stop=(k == KC - 1))
                        for k in range(KC):
                            ins = e.matmul(self.PS[pu][:, :], self.WIN[ws][:, k, 128:256], self.XN[:, k, ts],
                                           start=(k == 0), stop=(k == KC - 1))
                        return ins
                    sc.op(sc.pe, mm, reads=[self.bWIN[ws]] + [self.bXN[k][t] for k in range(KC)],
                          writes=[self.bPS[pg], self.bPS[pu]])
                    si = sgi
                    sgi = (sgi + 1) % 3
                    sc.op(sc.act, lambda e, pg=pg, si=si: e.activation(out=SG[si], in_=self.PS[pg][:, :], func=AF.Silu),
                          reads=[self.bPS[pg]], writes=[bSG[si]])
                    sc.op(sc.dve, lambda e, pu=pu, si=si, jj=jj, ts=ts: e.tensor_tensor(
                        out=ACTB[:, jj, ts], in0=SG[si], in1=self.PS[pu][:, :], op=ALU.mult),
                        reads=[bSG[si], self.bPS[pu]], writes=[bACT[jj][t]])
            for dm in range(KC):
                for t in range(NT):
                    ts = self.tsl(t)
                    po = self.next_ps()

                    def mm2(e, dm=dm, ts=ts, po=po, wo_slot=wo_slot, nj=nj):
                        for jj in range(nj):
                            ins = e.matmul(self.PS[po][:, :], self.WOUT[wo_slot][:, jj, dm * 128:(dm + 1) * 128],
                                           ACTB[:, jj, ts], start=(jj == 0), stop=(jj == nj - 1))
                        return ins
                    sc.op(sc.pe, mm2, reads=[self.bWOUT[wo_slot]] + [bACT[jj][t] for jj in range(nj)],
                          writes=[self.bPS[po]])
                    sc.op(sc.dve, lambda e, dm=dm, ts=ts, po=po: e.scalar_tensor_tensor(
                        out=self.X[:, dm, ts], in0=self.PS[po][:, :], scalar=0.5, in1=self.X[:, dm, ts],
                        op0=ALU.mult, op1=ALU.add),
                        reads=[self.bPS[po], self.bX[dm][t]], writes=[self.bX[dm][t]])

    def load_slot(self, kind, make_pairs):
        if kind == "win":
            i = self.win_i
            self.win_i = (self.win_i + 1) % 3
            self.sc.dma(self.sc.pool, make_pairs(self.WINF[i]), writes=[self.bWIN[i]])
        else:
            i = self.wout_i
            self.wout_i = (self.wout_i + 1) % 2
            self.sc.dma(self.sc.pool, make_pairs(self.WOUTF[i]), writes=[self.bWOUT[i]])
        return i

    def attn(self):
        sc = self.sc
        ar = self.ar
        ar.reset()
        wqkv = self.dram["wqkv"]
        wo = self.dram["wo_sb"]
        NEGTRI = self.CST[:, 0, :]
        MASK = self.CST[:, 1, :]
        BONES = self.CST[:, 2, :]

        def ld_qkv(hp):
            return self.load_slot("wout", lambda W: [
                (W[:, 0:3072].rearrange("p (k c) -> p k c", c=384), wqkv[hp][:, :, :])])

        def ld_wo(hp):
            return self.load_slot("win", lambda W: [(W[:, 0:D], wo[:, hp, :])])

        qkv_slot = {0: ld_qkv(0)}
        wo_slot = {0: ld_wo(0)}

        mark = ar.off
        self.rmsnorm(V_NORM["l0_mix_norm"])
        sc.barrier()
        ar.off = mark

        QA = [ar.alloc([S], BF16) for _ in range(2)]
        QB = [ar.alloc([S], BF16) for _ in range(2)]
        KT = [ar.alloc([S], BF16) for _ in range(2)]
        VV = [ar.alloc([16, 128], BF16) for _ in range(2)]
        OP = [ar.alloc([S], BF16) for _ in range(2)]
        bQ = [[Buf("q%d_%d" % (i, t)) for t in range(NT)] for i in range(2)]
        bK = [[Buf("k%d_%d" % (i, t)) for t in range(NT)] for i in range(2)]
        bV = [[Buf("v%d_%d" % (i, g)) for g in range(4)] for i in range(2)]
        bO = [[Buf("o%d_%d" % (i, t)) for t in range(NT)] for i in range(2)]
        SQ2 = [ar.alloc([TW], BF16) for _ in range(2)]
        bSQ2 = [Buf("sq2_%d" % i) for i in range(2)]
        TMP2 = [ar.alloc([TW], F32) for _ in range(2)]
        bTMP2 = [Buf("tmp2_%d" % i) for i in range(2)]
        RS2 = [ar.alloc([TW], F32) for _ in range(2)]
        bRS2 = [Buf("rs2_%d" % i) for i in range(2)]
        E = [ar.alloc([TW], F32) for _ in range(3)]
        bE = [Buf("e%d" % i) for i in range(3)]
        ND = 6
        SP = [ar.alloc([TW], BF16) for _ in range(ND)]
        bSP = [Buf("sp%d" % i) for i in range(ND)]
        SS = [ar.alloc([TW], BF16) for _ in range(2)]
        bSS = [Buf("ss%d" % i) for i in range(2)]
        WW = [ar.alloc([TW], BF16) for _ in range(ND)]
        bWW = [Buf("ww%d" % i) for i in range(ND)]

        GQ = self.SMALL[:, 0:1]
        sc.op(sc.dve, lambda e: e.tensor_scalar(out=GQ, in0=self.VEC[:, V_QG:V_QG + 1], scalar1=0.125, scalar2=None,
                                                op0=ALU.mult), reads=[self.bVEC], writes=[self.bSMALL])
        GK = self.VEC[:, V_KG:V_KG + 1]
        for i in range(2):
            sc.op(sc.dve, lambda e, i=i: e.memset(QA[i][64:128, :], 0.0), writes=bQ[i])
            sc.op(sc.dve, lambda e, i=i: e.memset(QB[i][0:64, :], 0.0), writes=bQ[i])

        PZ = [0, 1, 2, 3]
        POB = [4, 5]
        PJ = [6, 7]
        cnt = {"pj": 0, "n2": 0}

        def pj():
            cnt["pj"] += 1
            return PJ[cnt["pj"] % 2]

        def project_units(hp):
            st = hp % 2
            ws = qkv_slot[hp]
            WQ = self.WOUTF[ws][:, 0:3072].rearrange("p (k c) -> p k c", c=384)
            units = []
            for t in range(NT):
                for which in (0, 1):
                    units.append(lambda t=t, which=which: proj_qk(hp, st, ws, WQ, t, which))
            for g in range(4):
                units.append(lambda g=g: proj_v(hp, st, ws, WQ, g))
            return units

        def proj_qk(hp, st, ws, WQ, t, which):
            if True:
                ts = self.tsl(t)
                if True:
                    pr = pj()

                    def mm(e, pr=pr, which=which, ts=ts):
                        for k in range(KC):
                            ins = e.matmul(self.PS[pr][:, :], WQ[:, k, which * 128:(which + 1) * 128], self.XN[:, k, ts],
                                           start=(k == 0), stop=(k == KC - 1))
                        return ins
                    sc.op(sc.pe, mm, reads=[self.bWOUT[ws]] + [self.bXN[k][t] for k in range(KC)], writes=[self.bPS[pr]])
                    n2 = cnt["n2"] % 2
                    cnt["n2"] += 1
                    sc.op(sc.act, lambda e, pr=pr, n2=n2: e.activation(out=SQ2[n2], in_=self.PS[pr][:, :], func=AF.Square),
                          reads=[self.bPS[pr]], writes=[bSQ2[n2]])
                    p2 = pj()
                    sc.op(sc.pe, lambda e, p2=p2, n2=n2: e.matmul(self.PS[p2][:, :], BONES, SQ2[n2], start=True, stop=True),
                          reads=[bSQ2[n2], self.bCST], writes=[self.bPS[p2]])
                    sc.op(sc.act, lambda e, p2=p2, n2=n2: e.activation(out=TMP2[n2], in_=self.PS[p2][:, :], func=AF.Ln,
                                                                       scale=1.0 / 64, bias=EPS),
                          reads=[self.bPS[p2]], writes=[bTMP2[n2]])
                    sc.op(sc.act, lambda e, n2=n2: e.activation(out=RS2[n2], in_=TMP2[n2], func=AF.Exp, scale=-0.5),
                          reads=[bTMP2[n2]], writes=[bRS2[n2]])
                    if which == 0:
                        sc.op(sc.dve, lambda e, pr=pr, n2=n2, ts=ts: e.scalar_tensor_tensor(
                            out=QA[st][0:64, ts], in0=self.PS[pr][0:64, :], scalar=GQ[0:64, :], in1=RS2[n2][0:64, :],
                            op0=ALU.mult, op1=ALU.mult), reads=[self.bPS[pr], bRS2[n2], self.bSMALL], writes=[bQ[st][t]])
                        sc.op(sc.dve, lambda e, pr=pr, n2=n2, ts=ts: e.scalar_tensor_tensor(
                            out=QB[st][64:128, ts], in0=self.PS[pr][64:128, :], scalar=GQ[64:128, :], in1=RS2[n2][64:128, :],
                            op0=ALU.mult, op1=ALU.mult), reads=[self.bPS[pr], bRS2[n2], self.bSMALL], writes=[bQ[st][t]])
                    else:
                        sc.op(sc.dve, lambda e, pr=pr, n2=n2, ts=ts: e.scalar_tensor_tensor(
                            out=KT[st][:, ts], in0=self.PS[pr][:, :], scalar=GK, in1=RS2[n2],
                            op0=ALU.mult, op1=ALU.mult), reads=[self.bPS[pr], bRS2[n2], self.bVEC], writes=[bK[st][t]])
        def proj_v(hp, st, ws, WQ, g):
            if True:
                pr = pj()

                def mmv(e, pr=pr, g=g):
                    for q4 in range(4):
                        scn = g * 4 + q4
                        for k in range(KC):
                            ins = e.matmul(self.PS[pr][:, q4 * 128:(q4 + 1) * 128], self.XN[:, k, scn * 128:(scn + 1) * 128],
                                           WQ[:, k, 256:384], start=(k == 0), stop=(k == KC - 1))
                    return ins
                sc.op(sc.pe, mmv, reads=[self.bWOUT[ws]] + [self.bXN[k][g] for k in range(KC)], writes=[self.bPS[pr]])
                sc.op(sc.act, lambda e, pr=pr, g=g: e.activation(
                    out=VV[st][:, g * 4:(g + 1) * 4, :].rearrange("p a b -> p (a b)"), in_=self.PS[pr][:, :], func=AF.Copy),
                    reads=[self.bPS[pr]], writes=[bV[st][g]])

        def core(hp, pending):
            st = hp % 2
            items = []
            for t in range(NT):
                for c in range(4 * t + 3, -1, -1):
                    for hl in range(2):
                        items.append((hl, t, c))
            n = len(items)
            every = max(1, (n + 4) // (len(pending) + 1)) if pending else 0
            state = {}

            def stageA(i):
                hl, t, c = items[i]
                lo = max(0, 128 * (c - 4 * t))
                diag = c >= 4 * t
                first = (c == 4 * t + 3)
                gi = hl
                Qp = QA[st] if hl == 0 else QB[st]
                qs = slice(t * TW + lo, (t + 1) * TW)
                pz = PZ[i % 4]
                ei = i % 3
                si = i % ND
                sc.op(sc.pe, lambda e, pz=pz, c=c, lo=lo, qs=qs, Qp=Qp: e.matmul(
                    self.PS[pz][:, lo:TW], KT[st][:, c * 128:(c + 1) * 128], Qp[:, qs], start=True, stop=True),
                    reads=[bK[st][c // 4], bQ[st][t]], writes=[self.bPS[pz]])
                sc.op(sc.act, lambda e, pz=pz, ei=ei, lo=lo: e.activation(out=E[ei][:, lo:TW], in_=self.PS[pz][:, lo:TW],
                                                                          func=AF.Exp),
                      reads=[self.bPS[pz]], writes=[bE[ei]])
                sc.op(sc.act, lambda e, ei=ei, si=si, lo=lo: e.activation(out=SP[si][:, lo:TW], in_=E[ei][:, lo:TW],
                                                                          func=AF.Ln, bias=1.0),
                      reads=[bE[ei]], writes=[bSP[si]])
                if diag:
                    sc.op(sc.dve, lambda e, si=si, lo=lo: e.tensor_tensor(out=SP[si][:, lo:lo + 128], in0=SP[si][:, lo:lo + 128],
                                                                          in1=MASK, op=ALU.mult),
                          reads=[bSP[si], self.bCST], writes=[bSP[si]])

            def stageB(i):
                hl, t, c = items[i]
                lo = max(0, 128 * (c - 4 * t))
                diag = c >= 4 * t
                first = (c == 4 * t + 3)
                gi = hl
                Qp = QA[st] if hl == 0 else QB[st]
                qs = slice(t * TW + lo, (t + 1) * TW)
                pc = PZ[i % 4]
                si = i % ND
                wi = i % ND

                if first:
                    sc.op(sc.dve, lambda e, gi=gi: e.memset(SS[gi], 0.0), writes=[bSS[gi]])

                def mm(e):
                    ins = e.matmul(self.PS[pc][:, lo:TW], NEGTRI, SP[si][:, lo:TW], start=False, stop=first,
                                   skip_group_check=True)
                    if not first:
                        ins = e.matmul(self.PS[pc][:, lo:TW], self.NEGONES[:, :], SS[gi][:, lo:TW], start=False, stop=True,
                                       skip_group_check=True)
                    return ins
                sc.op(sc.pe, mm, reads=[bSP[si], bSS[gi], self.bCST, self.bONES, self.bPS[pc]],
                      writes=[self.bPS[pc]])
                sc.op(sc.act, lambda e: e.activation(out=WW[wi][:, lo:TW], in_=self.PS[pc][:, lo:TW], func=AF.Exp),
                      reads=[self.bPS[pc]], writes=[bWW[wi]])
                if diag:
                    sc.op(sc.dve, lambda e: e.tensor_tensor(out=WW[wi][:, lo:lo + 128], in0=WW[wi][:, lo:lo + 128],
                                                            in1=MASK, op=ALU.mult),
                          reads=[bWW[wi], self.bCST], writes=[bWW[wi]])
                if c > 0:
                    sc.op(sc.dve, lambda e: e.tensor_tensor(out=SS[gi][:, lo:TW], in0=SS[gi][:, lo:TW], in1=SP[si][:, lo:TW],
                                                            op=ALU.add),
                          reads=[bSS[gi], bSP[si]], writes=[bSS[gi]])

            def stageC(i):
                hl, t, c = items[i]
                lo = max(0, 128 * (c - 4 * t))
                first = (c == 4 * t + 3)
                gi = hl
                po = POB[gi]
                wi = i % ND
                sc.op(sc.pe, lambda e: e.matmul(self.PS[po][:, lo:TW], VV[st][:, c, :], WW[wi][:, lo:TW],
                                                start=first, stop=(c == 0), skip_group_check=True),
                      reads=[bV[st][c // 4], bWW[wi]], writes=[self.bPS[po]])
                if c == 0:
                    rows = slice(hl * 64, (hl + 1) * 64)
                    sc.op(sc.dve, lambda e: e.tensor_copy(out=OP[st][rows, self.tsl(t)], in_=self.PS[po][rows, :]),
                          reads=[self.bPS[po]], writes=[bO[st][t]])

            pend = list(pending)
            DB, DC = 2, 4
            for step in range(n + DC):
                if step < n:
                    stageA(step)
                if 0 <= step - DB < n:
                    stageB(step - DB)
                if 0 <= step - DC < n:
                    stageC(step - DC)
                if pend and every and step % every == every - 1:
                    pend.pop(0)()
            while pend:
                pend.pop(0)()

        def oproj_units(hp):
            return [lambda dm=dm: oproj(hp, dm) for dm in range(KC)]

        def oproj(hp, dm):
            st = hp % 2
            ws = wo_slot[hp]
            WO = self.WINF[ws]
            if True:
                for t in range(NT):
                    pr = pj()
                    sc.op(sc.pe, lambda e, pr=pr, dm=dm, t=t: e.matmul(self.PS[pr][:, :], WO[:, dm * 128:(dm + 1) * 128],
                                                                      OP[st][:, self.tsl(t)], start=True, stop=True),
                          reads=[self.bWIN[ws], bO[st][t]], writes=[self.bPS[pr]])
                    sc.op(sc.dve, lambda e, pr=pr, dm=dm, t=t: e.tensor_tensor(
                        out=self.X[:, dm, self.tsl(t)], in0=self.PS[pr][:, :], in1=self.X[:, dm, self.tsl(t)], op=ALU.add),
                        reads=[self.bPS[pr], self.bX[dm][t]], writes=[self.bX[dm][t]])

        for u in project_units(0):
            u()
        for hp in range(8):
            pending = []
            if hp + 1 < 8:
                qkv_slot[hp + 1] = ld_qkv(hp + 1)
                wo_slot[hp + 1] = ld_wo(hp + 1)
                pending += project_units(hp + 1)
            if hp >= 1:
                pending += oproj_units(hp - 1)
            core(hp, pending)
        for u in oproj_units(7):
            u()

    def lru(self):
        sc = self.sc
        ar = self.ar
        ar.reset()
        win = self.dram["lru_win"]
        wo = self.dram["lru_wo"]
        slots = {0: self.load_win(win[0]), 1: self.load_win(win[1])}
        mark = ar.off
        self.rmsnorm(V_NORM["l1_mix_norm"])
        sc.barrier()
        ar.off = mark
        wos = {}
        wos[0] = self.load_slot("wout", lambda W: [(W[:, 0:4 * D].rearrange("p (j d) -> p j d", d=D), wo[:, 0:4, :])])
        wos[1] = self.load_slot("wout", lambda W: [(W[:, 0:4 * D].rearrange("p (j d) -> p j d", d=D), wo[:, 4:8, :])])
        WR = self.WOUTF[wos[0]][:, 4096:5120].rearrange("p (a b) -> p a b", b=128)
        WI = self.WOUTF[wos[0]][:, 5120:6144].rearrange("p (a b) -> p a b", b=128)
        bWG = Buf("wgate")
        sc.op(sc.dve, lambda e: e.memset(WR, 0.0), writes=[bWG])
        sc.op(sc.dve, lambda e: e.memset(WI, 0.0), writes=[bWG])
        pairs = []
        for (Wt, nm) in ((WR, "lru_wr"), (WI, "lru_wi")):
            src = self.dram[nm]
            pairs.append((Wt[0:64, :, 0:64], src[0:64, :, :]))
            pairs.append((Wt[64:128, :, 64:128], src[64:128, :, :]))
        sc.dma(sc.pool, pairs, writes=[bWG])
        C1 = self.SMALL[:, 8:16]
        C2 = self.SMALL[:, 16:24]
        TS = self.SMALL[:, 24:32]
        sc.op(sc.act, lambda e: e.activation(out=TS, in_=self.VEC[:, V_LAM:V_LAM + 8], func=AF.Exp, scale=-1.0),
              reads=[self.bVEC], writes=[self.bSMALL])
        sc.op(sc.act, lambda e: e.activation(out=TS, in_=TS, func=AF.Ln, bias=1.0), reads=[self.bSMALL], writes=[self.bSMALL])
        sc.op(sc.dve, lambda e: e.tensor_scalar(out=C1, in0=TS, scalar1=-8.0, scalar2=None, op0=ALU.mult),
              reads=[self.bSMALL], writes=[self.bSMALL])
        sc.op(sc.dve, lambda e: e.tensor_scalar(out=C2, in0=TS, scalar1=-16.0, scalar2=None, op0=ALU.mult),
              reads=[self.bSMALL], writes=[self.bSMALL])

        PAD = 4
        B1 = ar.alloc([S + PAD], F32)
        B2 = ar.alloc([S], F32)
        B3 = ar.alloc([S], F32)
        B4 = ar.alloc([S], F32)
        b1, b2, b3, b4 = Buf("B1"), Buf("B2"), Buf("B3"), Buf("B4")
        HY = ar.alloc([KC, S], BF16)
        bHY = [Buf("hy%d" % j) for j in range(KC)]
        XCB = [ar.alloc([TW], BF16) for _ in range(2)]
        bXCB = [Buf("xcb%d" % i) for i in range(2)]
        IT = [ar.alloc([TW], F32) for _ in range(2)]
        bIT = [Buf("it%d" % i) for i in range(2)]
        sc.op(sc.dve, lambda e: e.memset(B1[:, 0:PAD], 0.0), writes=[b1])
        XB = B1[:, PAD:PAD + S]
        for j in range(KC):
            if j + 2 < KC:
                slots[j + 2] = self.load_win(win[j + 2])
            ws = slots[j]
            for t in range(NT):
                ts = self.tsl(t)
                px = self.next_ps()
                py = self.next_ps()

                def mm(e, ws=ws, ts=ts, px=px, py=py):
                    for k in range(KC):
                        e.matmul(self.PS[px][:, :], self.WIN[ws][:, k, 0:128], self.XN[:, k, ts], start=(k == 0), stop=(k == KC - 1))
                    for k in range(KC):
                        ins = e.matmul(self.PS[py][:, :], self.WIN[ws][:, k, 128:256], self.XN[:, k, ts], start=(k == 0),
                                       stop=(k == KC - 1))
                    return ins
                sc.op(sc.pe, mm, reads=[self.bWIN[ws]] + [self.bXN[k][t] for k in range(KC)], writes=[self.bPS[px], self.bPS[py]])
                sc.op(sc.act, lambda e, px=px, ts=ts: e.activation(out=XB[:, ts], in_=self.PS[px][:, :], func=AF.Copy),
                      reads=[self.bPS[px]], writes=[b1])
                sc.op(sc.act, lambda e, py=py, ts=ts: e.activation(out=B2[:, ts], in_=self.PS[py][:, :], func=AF.Copy),
                      reads=[self.bPS[py]], writes=[b2])
            sc.op(sc.dve, lambda e: e.tensor_tensor(out=B4, in0=B2, in1=B2, op=ALU.mult), reads=[b2], writes=[b4])
            sc.op(sc.dve, lambda e: e.tensor_scalar(out=B4, in0=B4, scalar1=0.044715, scalar2=1.0, op0=ALU.mult, op1=ALU.add),
                  reads=[b4], writes=[b4])
            sc.op(sc.dve, lambda e: e.tensor_tensor(out=B4, in0=B4, in1=B2, op=ALU.mult), reads=[b4, b2], writes=[b4])
            sc.op(sc.act, lambda e: e.activation(out=B4, in_=B4, func=AF.Sigmoid, scale=1.5957691216057308),
                  reads=[b4], writes=[b4])
            sc.op(sc.dve, lambda e: e.tensor_tensor(out=B2, in0=B2, in1=B4, op=ALU.mult), reads=[b4, b2], writes=[b2])
            cw = lambda tap, j=j: self.VEC[:, V_CONVW + tap * 8 + j: V_CONVW + tap * 8 + j + 1]
            cb = self.VEC[:, V_CONVB + j:V_CONVB + j + 1]
            sc.op(sc.dve, lambda e, cw=cw, cb=cb: e.tensor_scalar(out=B3, in0=B1[:, 1:1 + S], scalar1=cw(0), scalar2=cb,
                                                                  op0=ALU.mult, op1=ALU.add),
                  reads=[b1, self.bVEC], writes=[b3])
            for tap in (1, 2, 3):
                sc.op(sc.dve, lambda e, cw=cw, tap=tap: e.scalar_tensor_tensor(
                    out=B3, in0=B1[:, 1 + tap:1 + tap + S], scalar=cw(tap), in1=B3, op0=ALU.mult, op1=ALU.add),
                    reads=[b1, b3, self.bVEC], writes=[b3])
            for t in range(NT):
                ts = self.tsl(t)
                xi = t % 2
                sc.op(sc.act, lambda e, xi=xi, ts=ts: e.activation(out=XCB[xi], in_=B3[:, ts], func=AF.Copy),
                      reads=[b3], writes=[bXCB[xi]])
                pr = self.next_ps()
                pi = self.next_ps()
                sc.op(sc.pe, lambda e, pr=pr, xi=xi, j=j: e.matmul(self.PS[pr][:, :], WR[:, j, :], XCB[xi], start=True, stop=True),
                      reads=[bWG, bXCB[xi]], writes=[self.bPS[pr]])
                sc.op(sc.pe, lambda e, pi=pi, xi=xi, j=j: e.matmul(self.PS[pi][:, :], WI[:, j, :], XCB[xi], start=True, stop=True),
                      reads=[bWG, bXCB[xi]], writes=[self.bPS[pi]])
                sc.op(sc.act, lambda e, pr=pr, ts=ts, j=j: e.activation(out=XB[:, ts], in_=self.PS[pr][:, :], func=AF.Sigmoid,
                                                                        bias=self.VEC[:, V_BR + j:V_BR + j + 1]),
                      reads=[self.bPS[pr], b3, self.bVEC], writes=[b1])
                sc.op(sc.act, lambda e, pi=pi, xi=xi, j=j: e.activation(out=IT[xi], in_=self.PS[pi][:, :], func=AF.Sigmoid,
                                                                        bias=self.VEC[:, V_BI + j:V_BI + j + 1]),
                      reads=[self.bPS[pi], self.bVEC], writes=[bIT[xi]])
                sc.op(sc.dve, lambda e, xi=xi, ts=ts: e.tensor_tensor(out=B3[:, ts], in0=B3[:, ts], in1=IT[xi], op=ALU.mult),
                      reads=[b3, bIT[xi], bXCB[xi]], writes=[b3])
            sc.op(sc.act, lambda e, j=j: e.activation(out=B4, in_=XB, func=AF.Exp, scale=C2[:, j:j + 1]),
                  reads=[b1, self.bSMALL, b2], writes=[b4])
            sc.op(sc.act, lambda e, j=j: e.activation(out=XB, in_=XB, func=AF.Exp, scale=C1[:, j:j + 1]),
                  reads=[b1, self.bSMALL], writes=[b1])
            sc.op(sc.act, lambda e: e.activation(out=B4, in_=B4, func=AF.Sqrt, scale=-1.0, bias=1.0), reads=[b4], writes=[b4])
            sc.op(sc.dve, lambda e: e.tensor_tensor(out=B3, in0=B3, in1=B4, op=ALU.mult), reads=[b3, b4], writes=[b3])
            sc.op(sc.dve, lambda e: e.tensor_tensor_scan(out=B4, data0=XB, data1=B3, initial=0.0, op0=ALU.mult, op1=ALU.add),
                  reads=[b1, b3], writes=[b4])
            sc.op(sc.dve, lambda e, j=j: e.tensor_tensor(out=HY[:, j, :], in0=B4, in1=B2, op=ALU.mult),
                  reads=[b4, b2], writes=[bHY[j]])
        for dm in range(KC):
            for t in range(NT):
                ts = self.tsl(t)
                po = self.next_ps()

                def mm2(e, dm=dm, ts=ts, po=po):
                    for j in range(KC):
                        W = self.WOUTF[wos[j // 4]][:, 0:4 * D].rearrange("p (j d) -> p j d", d=D)
                        ins = e.matmul(self.PS[po][:, :], W[:, j % 4, dm * 128:(dm + 1) * 128], HY[:, j, ts],
                                       start=(j == 0), stop=(j == KC - 1))
                    return ins
                sc.op(sc.pe, mm2, reads=[self.bWOUT[wos[0]], self.bWOUT[wos[1]]] + bHY, writes=[self.bPS[po]])
                sc.op(sc.dve, lambda e, dm=dm, ts=ts, po=po: e.tensor_tensor(
                    out=self.X[:, dm, ts], in0=self.PS[po][:, :], in1=self.X[:, dm, ts], op=ALU.add),
                    reads=[self.bPS[po], self.bX[dm][t]], writes=[self.bX[dm][t]])


def _chunked(w):
    c = w.shape[1]
    return np.ascontiguousarray(w.reshape(KC, 128, c).transpose(1, 0, 2))


def _win_slabs(w, nslab, half_off):
    a = _chunked(w)
    out = np.empty((nslab, 128, KC, 256), np.float32)
    for j in range(nslab):
        out[j, :, :, 0:128] = a[:, :, j * 128:(j + 1) * 128]
        out[j, :, :, 128:256] = a[:, :, half_off + j * 128: half_off + (j + 1) * 128]
    return out


def _vec_cols(v):
    return np.ascontiguousarray(v.reshape(KC, 128).T)


def prepare_shared(inp):
    sh = {}
    vecs = np.zeros((128, NV), np.float32)
    for name, col in V_NORM.items():
        vecs[:, col:col + 8] = _vec_cols(inp[name])
    for j in range(4):
        vecs[:, V_CONVW + j * 8: V_CONVW + (j + 1) * 8] = _vec_cols(inp["l1_lru_conv_w"][j])
    vecs[:, V_CONVB:V_CONVB + 8] = _vec_cols(inp["l1_lru_conv_b"])
    vecs[:, V_BR:V_BR + 8] = _vec_cols(inp["l1_lru_b_r"])
    vecs[:, V_BI:V_BI + 8] = _vec_cols(inp["l1_lru_b_i"])
    vecs[:, V_LAM:V_LAM + 8] = _vec_cols(inp["l1_lru_lambda"])
    vecs[:, V_QG] = np.tile(inp["l0_sb_q_norm"], 2)
    vecs[:, V_KG] = np.tile(inp["l0_sb_k_norm"], 2)
    sh["vecs"] = vecs
    j = np.arange(128)[:, None]
    s = np.arange(128)[None, :]
    consts = np.zeros((128, 3, 128), np.float32)
    consts[:, 0, :] = -(j >= s).astype(np.float32)
    consts[:, 1, :] = (s > j).astype(np.float32)
    consts[:, 2, :] = ((j // 64) == (s // 64)).astype(np.float32)
    sh["consts"] = consts
    for l in (0, 1):
        for f in ("ff1", "ff2"):
            sh["l%d_%s_win" % (l, f)] = _win_slabs(inp["l%d_%s_w_in" % (l, f)], NFC, FF)
            wo = inp["l%d_%s_w_out" % (l, f)]
            sh["l%d_%s_wout" % (l, f)] = np.ascontiguousarray(wo.reshape(NFC, 128, D).transpose(1, 0, 2))
    a = _chunked(inp["l0_sb_w_qkv"])
    wqkv = np.empty((8, 128, KC, 384), np.float32)
    for hp in range(8):
        for w3 in range(3):
            wqkv[hp, :, :, w3 * 128:(w3 + 1) * 128] = a[:, :, w3 * 1024 + hp * 128: w3 * 1024 + (hp + 1) * 128]
    sh["wqkv"] = wqkv
    sh["wo_sb"] = _chunked(inp["l0_sb_w_o"])
    sh["lru_win"] = _win_slabs(inp["l1_lru_w_in"], 8, 1024)
    for nm, key in (("lru_wr", "l1_lru_w_r"), ("lru_wi", "l1_lru_w_i")):
        w = inp[key]
        o = np.empty((128, 8, 64), np.float32)
        for jj in range(8):
            o[0:64, jj, :] = w[2 * jj]
            o[64:128, jj, :] = w[2 * jj + 1]
        sh[nm] = o
    sh["lru_wo"] = _chunked(inp["l1_lru_w_o"])
    return sh


def x_to_dev(xb):
    return np.ascontiguousarray(xb.T.reshape(KC, 128, S).transpose(1, 0, 2))


def y_from_dev(y):
    return np.ascontiguousarray(y.transpose(1, 0, 2).reshape(D, S).T)


FULL_PHASES = [("ffn", 0, "ff1"), ("attn",), ("ffn", 0, "ff2"),
               ("ffn", 1, "ff1"), ("lru",), ("ffn", 1, "ff2")]

_PROG_CACHE = {}
FUSED = True


def run_phases(inp, phases, xs, ncores):
    key = tuple(phases)
    if key not in _PROG_CACHE:
        _PROG_CACHE[key] = Prog(phases)
    prog = _PROG_CACHE[key]
    sh = prepare_shared(inp)
    in_maps = []
    for c in range(ncores):
        m = dict(sh)
        m["x"] = x_to_dev(xs[c])
        in_maps.append(m)
    res = run_bass_kernel_spmd(prog.nc, in_maps, core_ids=list(range(ncores)))
    return [y_from_dev(np.asarray(r["y"])) for r in res.results]


def kernel(**inputs):
    inp = {k: np.asarray(v, dtype=np.float32) for k, v in inputs.items()}
    x = inp["x"]
    if FUSED:
        outs = run_phases(inp, FULL_PHASES, [x[b] for b in range(8)], 8)
    else:
        mid = run_phases(inp, FULL_PHASES[:3], [x[b] for b in range(8)], 8)
        outs = run_phases(inp, FULL_PHASES[3:], mid, 8)
    return np.stack(outs, axis=0).astype(np.float32)
```
